# Optimizing a Trainium2 kernel written in Bass

```python
import math
import jax, jax.numpy as jnp
from jax import lax
import numpy as np

D_MODEL = 4096
BATCH = 8
SEQ = 2048
DEPTH = 2

HEAD_DIM = 128
A_HEADS = (3 * D_MODEL) // (8 * HEAD_DIM)
A_KV_HEADS = A_HEADS // 3
A_GROUP = A_HEADS // A_KV_HEADS
A_WINDOW = 128
A_BLOCK = 128
POOL_GROUPS = 4
POOL_WIDTH = D_MODEL // 4
POOL_GROUP_WIDTH = POOL_WIDTH // POOL_GROUPS
POOL_WINDOWS = (2, 4, 8, 16)
C_HEADS = (3 * D_MODEL) // (8 * HEAD_DIM)
C_HALF_DIM = HEAD_DIM // 2
C_Q_BLOCK = 128
A_Q_W = A_HEADS * HEAD_DIM
A_KV_W = A_KV_HEADS * HEAD_DIM
C_W = C_HEADS * HEAD_DIM
MIX_WIDTH = A_Q_W + POOL_WIDTH + C_W
IN_SIZES = (A_Q_W, A_KV_W, A_KV_W, POOL_WIDTH, C_W, C_W, C_W)
IN_WIDTH = sum(IN_SIZES)
IN_SPLITS = tuple(int(s) for s in np.cumsum(IN_SIZES)[:-1])
BRANCH_SPLITS = (A_Q_W, A_Q_W + POOL_WIDTH)
N_BRANCHES = 3
ADA_CHUNKS = 6
FFN_HIDDEN = -(-8 * D_MODEL // (3 * 256)) * 256
NEG_INF = -1e30
NORM_EPS = 1e-6
SUBLN_EPS = 1e-5

kernel_name = "hybrid_parallel_gated_encoder"


def rms_norm(x, g, eps=NORM_EPS):
    xf = x.astype(jnp.float32)
    y = xf * lax.rsqrt(jnp.mean(xf * xf, axis=-1, keepdims=True) + eps)
    return (y * g.astype(jnp.float32)).astype(x.dtype)


def modulate(h, shift, scale):
    return h * (1 + scale[:, None, :]) + shift[:, None, :]


def alibi_slopes(n):
    return jnp.asarray(2.0 ** (-8.0 * np.arange(1, n + 1) / n), dtype=jnp.float32)


def windowed_gqa_sink(q, k, v, sink):
    B, T = q.shape[0], q.shape[1]
    nb = T // A_BLOCK
    qb = q.reshape(B, nb, A_BLOCK, A_KV_HEADS, A_GROUP, HEAD_DIM)
    pad = ((0, 0), (A_BLOCK, A_BLOCK), (0, 0), (0, 0))
    kp = jnp.pad(k, pad).reshape(B, nb + 2, A_BLOCK, A_KV_HEADS, HEAD_DIM)
    vp = jnp.pad(v, pad).reshape(B, nb + 2, A_BLOCK, A_KV_HEADS, HEAD_DIM)
    kb = jnp.concatenate([kp[:, :-2], kp[:, 1:-1], kp[:, 2:]], axis=2)
    vb = jnp.concatenate([vp[:, :-2], vp[:, 1:-1], vp[:, 2:]], axis=2)
    logits = jnp.einsum('bnqhgd,bnshd->bnhgqs', qb, kb,
                        preferred_element_type=jnp.float32) * (HEAD_DIM ** -0.5)
    qi = jnp.arange(A_BLOCK)
    kj = jnp.arange(3 * A_BLOCK)
    rel = qi[:, None] + A_BLOCK - kj[None, :]
    kpos = jnp.arange(nb)[:, None] * A_BLOCK - A_BLOCK + kj[None, :]
    mask = (jnp.abs(rel) <= A_WINDOW)[None] & ((kpos >= 0) & (kpos < T))[:, None, :]
    slopes = alibi_slopes(A_HEADS).reshape(A_KV_HEADS, A_GROUP)
    bias = -slopes[:, :, None, None] * jnp.abs(rel).astype(jnp.float32)
    logits = jnp.where(mask[None, :, None, None], logits + bias, NEG_INF)
    s = sink.astype(jnp.float32).reshape(A_KV_HEADS, A_GROUP)[None, None, :, :, None, None]
    m = jnp.maximum(jnp.max(logits, axis=-1, keepdims=True), s)
    p = jnp.exp(logits - m)
    probs = p / (jnp.sum(p, axis=-1, keepdims=True) + jnp.exp(s - m))
    o = jnp.einsum('bnhgqs,bnshd->bnqhgd', probs.astype(v.dtype), vb)
    return o.reshape(B, T, A_Q_W)


def multiscale_pool(u, pool_w, pool_scale):
    B, T = u.shape[0], u.shape[1]
    ug = u.reshape(B, T, POOL_GROUPS, POOL_GROUP_WIDTH).astype(jnp.float32)
    csum = jnp.pad(jnp.cumsum(ug, axis=1), ((0, 0), (1, 0), (0, 0), (0, 0)))
    pos = jnp.arange(T)
    means = []
    for g, w in enumerate(POOL_WINDOWS):
        r = w // 2
        lo = jnp.maximum(pos - r, 0)
        hi = jnp.minimum(pos + r + 1, T)
        sg = csum[:, :, g]
        cnt = (hi - lo).astype(jnp.float32)[None, :, None]
        means.append((sg[:, hi] - sg[:, lo]) / cnt)
    pooled = jnp.stack(means, axis=2)
    z = (pooled - ug).astype(u.dtype)
    z = jnp.einsum('btgc,gcd->btgd', z, pool_w)
    return z.reshape(B, T, POOL_WIDTH) * pool_scale


def diff_attention(q, k, v, lam, subln_g, lam_init):
    B, T = q.shape[0], q.shape[1]
    nb = T // C_Q_BLOCK
    qb = q.reshape(B, nb, C_Q_BLOCK, C_HEADS, 2, C_HALF_DIM).transpose(1, 0, 2, 3, 4, 5)
    slopes = alibi_slopes(C_HEADS)[:, None, None, None]
    kpos = jnp.arange(T)
    scale = C_HALF_DIM ** -0.5

    def block(args):
        qblk, n = args
        logits = jnp.einsum('bqhcd,bshcd->bhcqs', qblk, k,
                            preferred_element_type=jnp.float32) * scale
        qpos = n * C_Q_BLOCK + jnp.arange(C_Q_BLOCK)
        dist = jnp.abs(qpos[:, None] - kpos[None, :]).astype(jnp.float32)
        probs = jax.nn.softmax(logits - slopes * dist, axis=-1)
        wts = probs[:, :, 0] - lam * probs[:, :, 1]
        return jnp.einsum('bhqs,bshd->bqhd', wts.astype(v.dtype), v)

    o = lax.map(block, (qb, jnp.arange(nb)))
    o = o.transpose(1, 0, 2, 3, 4).reshape(B, T, C_HEADS, HEAD_DIM)
    o = rms_norm(o, subln_g, eps=SUBLN_EPS) * (1 - lam_init)
    return o.reshape(B, T, C_W)


def hybrid_mixer(h, w_in, sink, pool_w, pool_scale, lam_vecs, subln_g,
                 w_gate, b_gate, w_branch, w_out, lam_init):
    B, T = h.shape[0], h.shape[1]
    proj = jnp.einsum('btd,de->bte', h, w_in)
    qa, ka, va, u, qc, kc, vc = jnp.split(proj, IN_SPLITS, axis=-1)
    oa = windowed_gqa_sink(qa.reshape(B, T, A_HEADS, HEAD_DIM),
                           ka.reshape(B, T, A_KV_HEADS, HEAD_DIM),
                           va.reshape(B, T, A_KV_HEADS, HEAD_DIM), sink)
    ob = multiscale_pool(u, pool_w, pool_scale)
    lv = lam_vecs.astype(jnp.float32)
    lam = jnp.exp(jnp.sum(lv[0] * lv[1])) - jnp.exp(jnp.sum(lv[2] * lv[3])) + lam_init
    oc = diff_attention(qc.reshape(B, T, C_HEADS, 2, C_HALF_DIM),
                        kc.reshape(B, T, C_HEADS, 2, C_HALF_DIM),
                        vc.reshape(B, T, C_HEADS, HEAD_DIM), lam, subln_g, lam_init)
    gates = jax.nn.sigmoid(jnp.einsum('btd,de->bte', h, w_gate) + b_gate)
    gates = gates.reshape(B, T, N_BRANCHES, D_MODEL)
    wa, wb, wc = jnp.split(w_branch, BRANCH_SPLITS, axis=0)
    merged = (gates[:, :, 0] * jnp.einsum('btk,kd->btd', oa, wa)
              + gates[:, :, 1] * jnp.einsum('btk,kd->btd', ob, wb)
              + gates[:, :, 2] * jnp.einsum('btk,kd->btd', oc, wc))
    return jnp.einsum('btd,de->bte', merged, w_out)


def swiglu(h, w_gate, w_up, w_down):
    a = jnp.einsum('btd,df->btf', h, w_gate)
    b = jnp.einsum('btd,df->btf', h, w_up)
    return jnp.einsum('btf,fd->btd', jax.nn.silu(a) * b, w_down)


def setup_inputs(seed: int = 0) -> dict:
    key = jax.random.key(seed)
    ks = jax.random.split(key, 21)
    D = D_MODEL

    def nrm(k, shape, scale):
        return jax.random.normal(k, shape, jnp.float32) * scale

    return {
        'x': nrm(ks[0], (BATCH, SEQ, D), 1.0),
        'c': nrm(ks[1], (BATCH, D), 1.0),
        'ada_w': nrm(ks[2], (DEPTH, D, ADA_CHUNKS * D), 0.5 * D ** -0.5),
        'ada_b': nrm(ks[3], (DEPTH, ADA_CHUNKS * D), 0.02),
        'mix_pre_g': 1.0 + nrm(ks[4], (DEPTH, D), 0.05),
        'mix_post_g': 1.0 + nrm(ks[5], (DEPTH, D), 0.05),
        'w_in': nrm(ks[6], (DEPTH, D, IN_WIDTH), D ** -0.5),
        'attn_sink': nrm(ks[7], (DEPTH, A_HEADS), 0.5),
        'pool_w': nrm(ks[8], (DEPTH, POOL_GROUPS, POOL_GROUP_WIDTH, POOL_GROUP_WIDTH), POOL_GROUP_WIDTH ** -0.5),
        'pool_scale': 1.0 + nrm(ks[9], (DEPTH, POOL_WIDTH), 0.1),
        'diff_lambda': nrm(ks[10], (DEPTH, 4, C_HALF_DIM), 0.1),
        'diff_subln_g': 1.0 + nrm(ks[11], (DEPTH, HEAD_DIM), 0.05),
        'w_gate': nrm(ks[12], (DEPTH, D, N_BRANCHES * D), D ** -0.5),
        'b_gate': nrm(ks[13], (DEPTH, N_BRANCHES * D), 0.02),
        'w_branch': nrm(ks[14], (DEPTH, MIX_WIDTH, D), C_W ** -0.5),
        'w_out': nrm(ks[15], (DEPTH, D, D), D ** -0.5),
        'ffn_pre_g': 1.0 + nrm(ks[16], (DEPTH, D), 0.05),
        'ffn_post_g': 1.0 + nrm(ks[17], (DEPTH, D), 0.05),
        'ffn_w_gate': nrm(ks[18], (DEPTH, D, FFN_HIDDEN), D ** -0.5),
        'ffn_w_up': nrm(ks[19], (DEPTH, D, FFN_HIDDEN), D ** -0.5),
        'ffn_w_down': nrm(ks[20], (DEPTH, FFN_HIDDEN, D), FFN_HIDDEN ** -0.5),
    }


def reference(x, c, ada_w, ada_b, mix_pre_g, mix_post_g, w_in, attn_sink, pool_w,
              pool_scale, diff_lambda, diff_subln_g, w_gate, b_gate, w_branch, w_out,
              ffn_pre_g, ffn_post_g, ffn_w_gate, ffn_w_up, ffn_w_down):
    cond = jax.nn.silu(c)
    for l in range(DEPTH):
        lam_init = 0.8 - 0.6 * math.exp(-0.3 * l)
        mod = jnp.einsum('bd,de->be', cond, ada_w[l]) + ada_b[l]
        shift1, scale1, gate1, shift2, scale2, gate2 = jnp.split(mod, ADA_CHUNKS, axis=-1)
        h = modulate(rms_norm(x, mix_pre_g[l]), shift1, scale1)
        y = hybrid_mixer(h, w_in[l], attn_sink[l], pool_w[l], pool_scale[l], diff_lambda[l],
                         diff_subln_g[l], w_gate[l], b_gate[l], w_branch[l], w_out[l], lam_init)
        x = x + gate1[:, None, :] * rms_norm(y, mix_post_g[l])
        h = modulate(rms_norm(x, ffn_pre_g[l]), shift2, scale2)
        y = swiglu(h, ffn_w_gate[l], ffn_w_up[l], ffn_w_down[l])
        x = x + gate2[:, None, :] * rms_norm(y, ffn_post_g[l])
    return x
```

```python
import math
import numpy as np
import ml_dtypes
import concourse.bass as bass
import concourse.mybir as mybir
from concourse.bass_utils import run_bass_kernel_spmd

F32 = mybir.dt.float32
BF16 = mybir.dt.bfloat16
AF = mybir.ActivationFunctionType
ALU = mybir.AluOpType

D = 4096
T = 2048
KC = 32
FF = 11008
FC = 86
NCORES = 8
DEPTH = 2
HD = 128
NA = 12
NKV = 4
NCH = 12
R_QA, R_KA, R_VA, R_U, R_QC, R_KC, R_VC = 0, 1536, 2048, 2560, 3584, 5120, 6656
POOL_WINDOWS = (2, 4, 8, 16)
DOFF = 1920
DW = 4352
ENG = ("sp", "act", "pe", "dve", "pool")

V_PRE1, V_POST1, V_PRE2, V_POST2, V_BG, V_PS, V_AB, V_SUB, V_SINK, V_DL = 0, 32, 64, 96, 128, 224, 232, 424, 425, 437
VL = 437 + 256
V_C = 2 * VL
NV = V_C + 32


class Sem:
    def __init__(self, nc, name):
        self.h = nc.alloc_semaphore(name)
        self.v = 0


class Buf:
    def __init__(self, ap, dsem=None):
        self.ap = ap
        self.wr = None
        self.rd = {}
        self.dsem = dsem


class KB:
    def __init__(self, nc):
        self.nc = nc
        self.q = {e: [] for e in ENG}
        self.waited = {e: {} for e in ENG}
        self.sems = []
        self.esem = {e: self.sem("e_" + e) for e in ("act", "pe", "dve", "pool")}
        self.pe_pending = []
        self.dpool = [self.sem(f"d{i}") for i in range(40)]
        self.dnext = 0
        self.arena_t = nc.alloc_sbuf_tensor("arena", [128, 176 * 256], F32)
        self.aoff = 0

    def sem(self, name):
        s = Sem(self.nc, name)
        self.sems.append(s)
        return s

    def dsem(self):
        s = self.dpool[self.dnext % len(self.dpool)]
        self.dnext += 1
        return s

    def alloc(self, free_shape, dt, dma=False):
        n = int(np.prod(free_shape))
        nb = n * (4 if dt == F32 else 2)
        n32 = (nb + 31) // 32 * 8
        assert self.aoff + n32 <= 176 * 256, ("arena overflow", self.aoff, n32)
        ap = self.arena_t[:, self.aoff:self.aoff + n32]
        self.aoff += n32
        if dt != F32:
            ap = ap.bitcast(dt)
        ap = ap[:, :n]
        if len(free_shape) == 2:
            ap = ap.rearrange("p (a b) -> p a b", a=free_shape[0])
        elif len(free_shape) == 3:
            ap = ap.rearrange("p (a b c) -> p a b c", a=free_shape[0], b=free_shape[1])
        elif len(free_shape) == 4:
            ap = ap.rearrange("p (a b c d) -> p a b c d", a=free_shape[0], b=free_shape[1], c=free_shape[2])
        return Buf(ap, self.dsem() if dma else None)

    def wait(self, eng, tok):
        if tok is None:
            return
        s, v = tok
        w = self.waited[eng]
        if w.get(s, 0) >= v:
            return
        w[s] = v
        self.q[eng].append(lambda e: e.wait_ge(s.h, v))

    def _deps(self, eng, reads, writes, extra):
        for b in reads:
            self.wait(eng, b.wr)
        for b in writes:
            self.wait(eng, b.wr)
            for s_, v_ in b.rd.items():
                self.wait(eng, (s_, v_))
        for t in extra:
            self.wait(eng, t)

    def _commit(self, tok, reads, writes):
        for b in reads:
            b.rd[tok[0]] = max(b.rd.get(tok[0], 0), tok[1])
        for b in writes:
            b.wr = tok
            b.rd = {}

    def op(self, eng, fn, reads=(), writes=(), extra=()):
        self._deps(eng, reads, writes, extra)
        sem = self.esem[eng]
        sem.v += 1
        v = sem.v
        self.q[eng].append(lambda e: fn(e).then_inc(sem.h, 1))
        tok = (sem, v)
        self._commit(tok, reads, writes)
        return tok

    def mm(self, fn, reads=(), writes=(), inc=False, extra=()):
        self._deps("pe", reads, writes, extra)
        self.pe_pending.append((list(reads), list(writes)))
        if not inc:
            self.q["pe"].append(fn)
            return None
        sem = self.esem["pe"]
        sem.v += 1
        v = sem.v
        self.q["pe"].append(lambda e: fn(e).then_inc(sem.h, 1))
        tok = (sem, v)
        for r, w in self.pe_pending:
            self._commit(tok, r, w)
        self.pe_pending = []
        return tok

    def dma(self, eng, pairs, reads=(), writes=(), sem=None, extra=()):
        self._deps(eng, reads, writes, extra)
        if sem is None:
            sem = (list(writes) + list(reads))[0].dsem
        for (o, i) in pairs:
            sem.v += 16
            self.q[eng].append(lambda e, o=o, i=i: e.dma_start(out=o, in_=i).then_inc(sem.h, 16))
        tok = (sem, sem.v)
        self._commit(tok, reads, writes)
        return tok

    def barrier(self):
        assert not self.pe_pending
        toks = [(s, s.v) for s in self.sems if s.v > 0]
        for e in ENG:
            for t in toks:
                self.wait(e, t)
        self.aoff = 0


def build_program(depth=DEPTH, debug=False):
    nc = bass.Bass("TRN2", target_bir_lowering=False)
    kb = KB(nc)

    def din(name, shape, dt):
        return nc.dram_tensor(name, list(shape), dt, kind="ExternalInput").ap()

    def dscr(name, shape, dt):
        kind = "ExternalOutput" if debug else "Internal"
        return nc.dram_tensor(name, list(shape), dt, kind=kind).ap()

    xin = din("xT", [D, T], F32)
    vecs = din("vecs", [128, NV], F32)
    ident_in = din("ident", [128, 128], BF16)
    dist_in = din("dist", [128, DW], F32)
    biasA_in = din("biasA", [128, NA * 384], F32)
    mt_in = din("mt", [4, 4, 128, 6 * 2 * 512], BF16)
    W = {}
    for nm, shp in (("ada_w", [D, 6 * D]), ("w_in", [D, 8192]), ("w_gate", [D, 3 * D]), ("w_branch", [D, D]),
                    ("w_out", [D, D]), ("ffn_w_gate", [D, FF]), ("ffn_w_up", [D, FF]), ("ffn_w_down", [FF, D])):
        W[nm] = [din(f"{nm}{l}", shp, F32) for l in range(depth)]
    pool_w = [din(f"pool_w{l}", [4, 256, 256], F32) for l in range(depth)]
    out = nc.dram_tensor("out", [D, T], F32, kind="ExternalOutput").ap()

    xres = dscr("xres", [D, T], F32)
    hT = dscr("hT", [D, T], BF16)
    projT = dscr("projT", [8192, T], BF16)
    gatesT = dscr("gatesT", [3 * D, T], BF16)
    oT = dscr("oT", [D, T], BF16)
    mergedT = dscr("mergedT", [D, T], BF16)
    yT = dscr("yT", [D, T], F32)
    y1T = dscr("y1T", [D, T], F32)
    sT = dscr("sT", [FF, T], BF16)

    def pers(name, shape, dt):
        return Buf(nc.alloc_sbuf_tensor(name, shape, dt)[:], None)

    VEC = pers("VEC", [128, NV], F32)
    VEC.dsem = kb.sem("vecd")
    IDENT = pers("IDENT", [128, 128], BF16)
    IDENT.dsem = kb.sem("identd")
    ONES_D = pers("ONES_D", [128, 128], BF16)
    ONES_H = pers("ONES_H", [128, 128], BF16)
    ONES_1 = pers("ONES_1", [128, 128], BF16)
    EPS = pers("EPS", [128, 2], F32)
    MOD = [pers(f"MOD{l}", [128, 192], F32) for l in range(depth)]
    DER = [pers(f"DER{l}", [128, 4 * 32 + 16 + 8], F32) for l in range(depth)]
    COND = pers("COND", [128, KC], BF16)
    SCR = pers("SCR", [128, 64], F32)
    PS = [Buf(nc.alloc_psum_tensor(f"ps{i}", [128, 512], F32)[:], None) for i in range(8)]

    def v(l, off, n=1):
        return VEC.ap[:, l * VL + off: l * VL + off + n]

    kb.dma("sp", [(VEC.ap, vecs)], writes=[VEC])
    kb.dma("sp", [(IDENT.ap, ident_in)], writes=[IDENT])
    kb.op("pool", lambda e: e.memset(ONES_D.ap, 1.0 / D), writes=[ONES_D])
    kb.op("pool", lambda e: e.memset(ONES_H.ap, 1.0 / HD), writes=[ONES_H])
    kb.op("pool", lambda e: e.memset(ONES_1.ap, 1.0), writes=[ONES_1])
    kb.op("pool", lambda e: e.memset(EPS.ap[:, 0:1], 1e-6), writes=[EPS])
    kb.op("pool", lambda e: e.memset(EPS.ap[:, 1:2], 1e-5), writes=[EPS])
    kb.op("act", lambda e: e.activation(out=COND.ap, in_=VEC.ap[:, V_C:V_C + 32], func=AF.Silu),
          reads=[VEC], writes=[COND])
    kb.barrier()

    def gemm(a_ap, kcn, tw, wblocks, epi, ksplits=None, setup=None):
        nh = max(1, tw // 512)
        hw = min(tw, 512)
        if ksplits is None:
            ksplits = [0, kcn]
        ng = len(ksplits) - 1
        if tw > 1:
            A = kb.alloc([kcn, tw], BF16, dma=True)
            av = a_ap.rearrange("(kc p) t -> p kc t", p=128)
            step = 8
            kb.dma("sp", [(A.ap[:, k0:min(k0 + step, kcn), :], av[:, k0:min(k0 + step, kcn), :])
                          for k0 in range(0, kcn, step)], writes=[A])
        else:
            A = a_ap
        WB = [kb.alloc([kcn, 256], BF16, dma=True) for _ in range(3)]
        ctx = setup() if setup else None
        bank_i = 0
        for b, (w_ap, meta) in enumerate(wblocks):
            wb = WB[b % 3]
            wv = w_ap.rearrange("(kc p) n -> p kc n", p=128)
            q = (kcn + 3) // 4
            kb.dma("pool", [(wb.ap[:, k0:min(k0 + q, kcn), :], wv[:, k0:min(k0 + q, kcn), :])
                            for k0 in range(0, kcn, q)], writes=[wb])
            for j in range(2):
                banks = []
                for h in range(nh):
                    bl = []
                    for g in range(ng):
                        bl.append(PS[bank_i % 8])
                        bank_i += 1
                    banks.append(bl)
                toks = [None] * nh
                for g in range(ng):
                    k0, k1 = ksplits[g], ksplits[g + 1]
                    for kc in range(k0, k1):
                        for h in range(nh):
                            last = (kc == k1 - 1)
                            fin = last and (g == ng - 1)
                            pb = banks[h][g]
                            t = kb.mm(lambda e, pb=pb, wb=wb, kc=kc, j=j, h=h, k0=k0, last=last:
                                      e.matmul(pb.ap[:, 0:hw], wb.ap[:, kc, j * 128:(j + 1) * 128],
                                               A.ap[:, kc, h * hw:(h + 1) * hw] if tw > 1 else A.ap[:, kc:kc + 1],
                                               start=(kc == k0), stop=last),
                                      reads=[wb, A], writes=[pb], inc=fin)
                            if fin:
                                toks[h] = t
                for h in range(nh):
                    epi(ctx, meta, j, h, banks[h])

    def make_ring(n, shape, dt):
        return {"b": [kb.alloc(shape, dt, dma=True) for _ in range(n)], "i": 0}

    def ring_next(r):
        b = r["b"][r["i"] % len(r["b"])]
        r["i"] += 1
        return b

    for l in range(depth):
        def epi_ada(ctx, meta, j, h, banks, l=l):
            col = meta * 2 + j
            kb.op("act", lambda e: e.activation(out=MOD[l].ap[:, col:col + 1], in_=banks[0].ap[:, 0:1],
                                                func=AF.Identity, bias=v(l, V_AB + col), scale=1.0),
                  reads=[banks[0], VEC], writes=[MOD[l]])
        gemm(COND, KC, 1, [(W["ada_w"][l][:, m * 256:(m + 1) * 256], m) for m in range(96)], epi_ada)
        kb.barrier()
        def derive(l):
            dr = DER[l]
            for (dst, sc, g) in ((0, 32, V_PRE1), (64, 128, V_PRE2)):
                kb.op("dve", lambda e, dst=dst, sc=sc, g=g: e.scalar_tensor_tensor(
                    out=dr.ap[:, dst:dst + 32], in0=MOD[l].ap[:, sc:sc + 32], scalar=1.0, in1=v(l, g, 32),
                    op0=ALU.add, op1=ALU.mult), reads=[MOD[l], VEC], writes=[dr])
            for (dst, gt, g) in ((32, 64, V_POST1), (96, 160, V_POST2)):
                kb.op("dve", lambda e, dst=dst, gt=gt, g=g: e.tensor_tensor(
                    out=dr.ap[:, dst:dst + 32], in0=MOD[l].ap[:, gt:gt + 32], in1=v(l, g, 32), op=ALU.mult),
                    reads=[MOD[l], VEC], writes=[dr])
            kb.op("act", lambda e: e.activation(out=dr.ap[:, 128:140], in_=v(l, V_SINK, 12), func=AF.Exp),
                  reads=[VEC], writes=[dr])
            lam_init = 0.8 - 0.6 * math.exp(-0.3 * l)
            kb.op("dve", lambda e: e.tensor_tensor(out=SCR.ap[:, 0:64], in0=v(l, V_DL, 64), in1=v(l, V_DL + 64, 64),
                                                   op=ALU.mult), reads=[VEC], writes=[SCR])
            kb.op("dve", lambda e: e.reduce_sum(out=dr.ap[:, 144:145], in_=SCR.ap[:, 0:64], axis=mybir.AxisListType.X),
                  reads=[SCR], writes=[dr])
            kb.op("dve", lambda e: e.tensor_tensor(out=SCR.ap[:, 0:64], in0=v(l, V_DL + 128, 64), in1=v(l, V_DL + 192, 64),
                                                   op=ALU.mult), reads=[VEC, dr], writes=[SCR])
            kb.op("dve", lambda e: e.reduce_sum(out=dr.ap[:, 145:146], in_=SCR.ap[:, 0:64], axis=mybir.AxisListType.X),
                  reads=[SCR], writes=[dr])
            kb.op("act", lambda e: e.activation(out=dr.ap[:, 146:148], in_=dr.ap[:, 144:146], func=AF.Exp),
                  reads=[dr], writes=[dr])
            kb.op("dve", lambda e: e.tensor_tensor(out=dr.ap[:, 148:149], in0=dr.ap[:, 147:148], in1=dr.ap[:, 146:147],
                                                   op=ALU.subtract), reads=[dr], writes=[dr])
            kb.op("dve", lambda e, li=lam_init: e.tensor_scalar(out=dr.ap[:, 148:149], in0=dr.ap[:, 148:149], scalar1=-li,
                                                                scalar2=None, op0=ALU.add), reads=[dr], writes=[dr])
            kb.op("dve", lambda e, li=lam_init: e.tensor_scalar(out=dr.ap[:, 149:150], in0=v(l, V_SUB), scalar1=1.0 - li,
                                                                scalar2=None, op0=ALU.mult), reads=[VEC, dr], writes=[dr])
            kb.barrier()
        derive(l)

    def norm_stage(l_post, post, y_src, x_src, x_dst, l_pre, pre):
        def tile(tt):
            ts = slice(tt * 512, (tt + 1) * 512)
            X = kb.alloc([KC, 512], F32, dma=True)
            xv = x_src.rearrange("(kc p) t -> p kc t", p=128)
            kb.dma("sp", [(X.ap[:, k0:k0 + 8, :], xv[:, k0:k0 + 8, ts]) for k0 in range(0, KC, 8)], writes=[X])
            SQ = [kb.alloc([512], BF16) for _ in range(2)]
            TMP = [kb.alloc([512], F32) for _ in range(2)]
            RS = kb.alloc([512], F32)
            pss = PS[tt % 2]

            def rstd_of(SRC):
                for kc in range(KC):
                    sq = SQ[kc % 2]
                    kb.op("act", lambda e, sq=sq, kc=kc: e.activation(out=sq.ap, in_=SRC.ap[:, kc, :], func=AF.Square),
                          reads=[SRC], writes=[sq])
                    kb.mm(lambda e, sq=sq, kc=kc: e.matmul(pss.ap, ONES_D.ap, sq.ap, start=(kc == 0), stop=(kc == KC - 1)),
                          reads=[sq, ONES_D], writes=[pss], inc=True)
                kb.op("act", lambda e: e.activation(out=RS.ap, in_=pss.ap, func=AF.Sqrt, bias=EPS.ap[:, 0:1], scale=1.0),
                      reads=[pss, EPS], writes=[RS])
                kb.op("dve", lambda e: e.reciprocal(out=RS.ap, in_=RS.ap), reads=[RS], writes=[RS])

            if post:
                Y = kb.alloc([KC, 512], F32, dma=True)
                yv = y_src.rearrange("(kc p) t -> p kc t", p=128)
                kb.dma("act", [(Y.ap[:, k0:k0 + 8, :], yv[:, k0:k0 + 8, ts]) for k0 in range(0, KC, 8)], writes=[Y])
                rstd_of(Y)
                gp = DER[l_post].ap[:, (32 if post == 1 else 96):]
                for kc in range(KC):
                    tm = TMP[kc % 2]
                    kb.op("dve", lambda e, tm=tm, kc=kc: e.tensor_tensor(out=tm.ap, in0=Y.ap[:, kc, :], in1=RS.ap, op=ALU.mult),
                          reads=[Y, RS], writes=[tm])
                    kb.op("dve", lambda e, tm=tm, kc=kc: e.scalar_tensor_tensor(
                        out=X.ap[:, kc, :], in0=tm.ap, scalar=gp[:, kc:kc + 1], in1=X.ap[:, kc, :],
                        op0=ALU.mult, op1=ALU.add), reads=[tm, DER[l_post], X], writes=[X])
                xo = x_dst.rearrange("(kc p) t -> p kc t", p=128)
                kb.dma("sp", [(xo[:, k0:k0 + 8, ts], X.ap[:, k0:k0 + 8, :]) for k0 in range(0, KC, 8)], reads=[X])
            if pre:
                H = kb.alloc([KC, 512], BF16, dma=True)
                rstd_of(X)
                go = 0 if pre == 1 else 64
                sh = MOD[l_pre].ap[:, (0 if pre == 1 else 96):]
                g_ = DER[l_pre].ap[:, go:]
                for kc in range(KC):
                    tm = TMP[kc % 2]
                    kb.op("dve", lambda e, tm=tm, kc=kc: e.tensor_tensor(out=tm.ap, in0=X.ap[:, kc, :], in1=RS.ap, op=ALU.mult),
                          reads=[X, RS], writes=[tm])
                    kb.op("act", lambda e, tm=tm, kc=kc: e.activation(out=H.ap[:, kc, :], in_=tm.ap, func=AF.Identity,
                                                                    scale=g_[:, kc:kc + 1], bias=sh[:, kc:kc + 1]),
                          reads=[tm, DER[l_pre], MOD[l_pre]], writes=[H])
                hv = hT.rearrange("(kc p) t -> p kc t", p=128)
                kb.dma("sp", [(hv[:, k0:k0 + 8, ts], H.ap[:, k0:k0 + 8, :]) for k0 in range(0, KC, 8)], reads=[H])
            kb.barrier()
        for tt in range(4):
            tile(tt)

    def store(dst_ap, ob):
        kb.dma("sp", [(dst_ap, ob.ap)], reads=[ob])

    def gemm_proj_gates(l, tt):
        t0 = tt * 1024

        def setup():
            return make_ring(4, [512], BF16)

        def epi(ring, meta, j, h, banks):
            kind, m = meta
            ob = ring_next(ring)
            row = (m * 2 + j) * 128
            cols = slice(t0 + h * 512, t0 + (h + 1) * 512)
            if kind == "p":
                kb.op("act", lambda e: e.activation(out=ob.ap, in_=banks[0].ap, func=AF.Copy),
                      reads=[banks[0]], writes=[ob])
                store(projT[row:row + 128, cols], ob)
            else:
                bcol = V_BG + m * 2 + j
                kb.op("act", lambda e: e.activation(out=ob.ap, in_=banks[0].ap, func=AF.Sigmoid,
                                                    bias=v(l, bcol), scale=1.0),
                      reads=[banks[0], VEC], writes=[ob])
                store(gatesT[row:row + 128, cols], ob)
        blocks = [(W["w_in"][l][:, m * 256:(m + 1) * 256], ("p", m)) for m in range(32)]
        blocks += [(W["w_gate"][l][:, m * 256:(m + 1) * 256], ("g", m)) for m in range(48)]
        gemm(hT[:, t0:t0 + 1024], KC, 1024, blocks, epi, setup=setup)
        kb.barrier()

    def gemm_branch(l, tt):
        t0 = tt * 1024

        def setup():
            return (make_ring(4, [512], BF16), make_ring(2, [3, 512], BF16), [kb.alloc([512], F32) for _ in range(2)])

        gv = gatesT.rearrange("(i r) t -> r i t", i=3)

        def epi(ctx, meta, j, h, banks):
            ring, gring, tmp = ctx
            ob = ring_next(ring)
            G = ring_next(gring)
            row = (meta * 2 + j) * 128
            cols = slice(t0 + h * 512, t0 + (h + 1) * 512)
            kb.dma("act", [(G.ap, gv[row:row + 128, :, cols])], writes=[G])
            a, b_ = tmp
            kb.op("dve", lambda e: e.tensor_tensor(out=a.ap, in0=banks[0].ap, in1=G.ap[:, 0, :], op=ALU.mult),
                  reads=[banks[0], G], writes=[a])
            kb.op("dve", lambda e: e.tensor_tensor(out=b_.ap, in0=banks[1].ap, in1=G.ap[:, 1, :], op=ALU.mult),
                  reads=[banks[1], G], writes=[b_])
            kb.op("dve", lambda e: e.tensor_tensor(out=a.ap, in0=a.ap, in1=b_.ap, op=ALU.add),
                  reads=[a, b_], writes=[a])
            kb.op("dve", lambda e: e.tensor_tensor(out=b_.ap, in0=banks[2].ap, in1=G.ap[:, 2, :], op=ALU.mult),
                  reads=[banks[2], G], writes=[b_])
            kb.op("dve", lambda e: e.tensor_tensor(out=ob.ap, in0=a.ap, in1=b_.ap, op=ALU.add),
                  reads=[a, b_], writes=[ob])
            store(mergedT[row:row + 128, cols], ob)
        blocks = [(W["w_branch"][l][:, m * 256:(m + 1) * 256], m) for m in range(16)]
        gemm(oT[:, t0:t0 + 1024], KC, 1024, blocks, epi, ksplits=[0, 12, 20, 32], setup=setup)
        kb.barrier()

    def gemm_plain_f32(w_ap_fn, nblk, a_ap, kcn, dst, t0, add_src=None):
        def setup():
            return (make_ring(4, [512], F32), make_ring(2, [512], F32))

        def epi(ctx, meta, j, h, banks):
            ring, lring = ctx
            ob = ring_next(ring)
            row = (meta * 2 + j) * 128
            cols = slice(t0 + h * 512, t0 + (h + 1) * 512)
            if add_src is None:
                kb.op("act", lambda e: e.activation(out=ob.ap, in_=banks[0].ap, func=AF.Copy),
                      reads=[banks[0]], writes=[ob])
            else:
                L = ring_next(lring)
                kb.dma("act", [(L.ap, add_src[row:row + 128, cols])], writes=[L])
                kb.op("dve", lambda e: e.tensor_tensor(out=ob.ap, in0=banks[0].ap, in1=L.ap, op=ALU.add),
                      reads=[banks[0], L], writes=[ob])
            store(dst[row:row + 128, cols], ob)
        blocks = [(w_ap_fn(m), m) for m in range(nblk)]
        gemm(a_ap, kcn, 1024, blocks, epi, setup=setup)
        kb.barrier()

    def gemm_ffn_up(l, tt):
        t0 = tt * 1024

        def setup():
            return (make_ring(4, [512], BF16), [kb.alloc([512], F32) for _ in range(4)], {})

        def epi(ctx, meta, j, h, banks):
            ring, sa, st = ctx
            kind, m = meta
            cols = slice(t0 + h * 512, t0 + (h + 1) * 512)
            if kind == "a":
                s_ = sa[j * 2 + h]
                kb.op("act", lambda e: e.activation(out=s_.ap, in_=banks[0].ap, func=AF.Silu),
                      reads=[banks[0]], writes=[s_])
            else:
                s_ = sa[j * 2 + h]
                ob = ring_next(ring)
                row = (m * 2 + j) * 128
                kb.op("dve", lambda e: e.tensor_tensor(out=ob.ap, in0=banks[0].ap, in1=s_.ap, op=ALU.mult),
                      reads=[banks[0], s_], writes=[ob])
                store(sT[row:row + 128, cols], ob)
        blocks = []
        for m in range(43):
            blocks.append((W["ffn_w_gate"][l][:, m * 256:(m + 1) * 256], ("a", m)))
            blocks.append((W["ffn_w_up"][l][:, m * 256:(m + 1) * 256], ("b", m)))
        gemm(hT[:, t0:t0 + 1024], KC, 1024, blocks, epi, setup=setup)
        kb.barrier()

    def transpose_to(dst, src_ap_fn, n, psb, dst_ap=None):
        if dst_ap is None:
            dst_ap = dst.ap
        for i0 in range(0, n, 8):
            cnt = min(8, n - i0)
            pv = psb.ap.bitcast(BF16)
            for i in range(cnt):
                kb.mm(lambda e, i=i, i0=i0: e.transpose(pv[:, i * 128:(i + 1) * 128], src_ap_fn(i0 + i), IDENT.ap),
                      reads=[IDENT] + src_ap_fn.bufs, writes=[psb], inc=(i == cnt - 1))
            kb.op("dve", lambda e, i0=i0, cnt=cnt: e.tensor_copy(
                out=dst_ap[:, i0:i0 + cnt, :], in_=pv[:, 0:cnt * 128].rearrange("p (a b) -> p a b", a=cnt)),
                reads=[psb], writes=[dst])

    def mixer_stage(l):
        dr = DER[l]
        scaleA = HD ** -0.5
        scaleC = 64 ** -0.5
        def groupA(g):
            BIAS = kb.alloc([NA, 384], F32, dma=True)
            kb.dma("sp", [(BIAS.ap, biasA_in.rearrange("p (h c) -> p h c", h=NA))], writes=[BIAS])
            KT = kb.alloc([T], BF16, dma=True)
            VT = kb.alloc([T], BF16, dma=True)
            Vt = kb.alloc([16, 128], BF16)
            kb.dma("sp", [(KT.ap, projT[R_KA + g * 128:R_KA + (g + 1) * 128, :])], writes=[KT])
            kb.dma("sp", [(VT.ap, projT[R_VA + g * 128:R_VA + (g + 1) * 128, :])], writes=[VT])
            f = lambda i: VT.ap[:, i * 128:(i + 1) * 128]
            f.bufs = [VT]
            transpose_to(Vt, f, 16, PS[7])
            def headA(hd):
                QT = kb.alloc([T], BF16, dma=True)
                OA = kb.alloc([T], BF16, dma=True)
                kb.dma("sp", [(QT.ap, projT[R_QA + hd * 128:R_QA + (hd + 1) * 128, :])], writes=[QT])
                EIN = [kb.alloc([384], F32) for _ in range(2)]
                EE = [kb.alloc([384], BF16) for _ in range(2)]
                DEN = [kb.alloc([128], F32) for _ in range(2)]
                for n in range(16):
                    slots = [s for s in range(3) if 0 <= n - 1 + s < 16]
                    c0, c1 = slots[0] * 128, (slots[-1] + 1) * 128
                    pS = PS[n % 2]
                    pO = PS[2 + n % 2]
                    ein, ee, den = EIN[n % 2], EE[n % 2], DEN[n % 2]
                    for s in slots:
                        jb = n - 1 + s
                        kb.mm(lambda e, s=s, jb=jb, n=n, pS=pS: e.matmul(
                            pS.ap[:, s * 128:(s + 1) * 128], KT.ap[:, jb * 128:(jb + 1) * 128],
                            QT.ap[:, n * 128:(n + 1) * 128], start=True, stop=True),
                            reads=[KT, QT], writes=[pS], inc=(s == slots[-1]))
                    kb.op("dve", lambda e, pS=pS, ein=ein, c0=c0, c1=c1, hd=hd: e.scalar_tensor_tensor(
                        out=ein.ap[:, c0:c1], in0=pS.ap[:, c0:c1], scalar=scaleA, in1=BIAS.ap[:, hd, c0:c1],
                        op0=ALU.mult, op1=ALU.add), reads=[pS, BIAS], writes=[ein])
                    kb.op("act", lambda e, ein=ein, ee=ee, c0=c0, c1=c1: e.activation(
                        out=ee.ap[:, c0:c1], in_=ein.ap[:, c0:c1], func=AF.Exp), reads=[ein], writes=[ee])
                    for s in slots:
                        jb = n - 1 + s
                        kb.mm(lambda e, s=s, jb=jb, pO=pO, ee=ee, slots=slots: e.matmul(
                            pO.ap[:, 0:128], Vt.ap[:, jb, :], ee.ap[:, s * 128:(s + 1) * 128],
                            start=(s == slots[0]), stop=(s == slots[-1])), reads=[Vt, ee], writes=[pO])
                    for s in slots:
                        kb.mm(lambda e, s=s, pO=pO, ee=ee, slots=slots: e.matmul(
                            pO.ap[:, 128:256], ONES_1.ap, ee.ap[:, s * 128:(s + 1) * 128],
                            start=(s == slots[0]), stop=(s == slots[-1])), reads=[ONES_1, ee], writes=[pO],
                            inc=(s == slots[-1]))
                    kb.op("dve", lambda e, pO=pO, den=den, hd=hd: e.tensor_scalar(
                        out=den.ap, in0=pO.ap[:, 128:256], scalar1=dr.ap[:, 128 + hd:129 + hd], scalar2=None,
                        op0=ALU.add), reads=[pO, dr], writes=[den])
                    kb.op("dve", lambda e, den=den: e.reciprocal(out=den.ap, in_=den.ap), reads=[den], writes=[den])
                    kb.op("dve", lambda e, pO=pO, den=den, n=n: e.tensor_tensor(
                        out=OA.ap[:, n * 128:(n + 1) * 128], in0=pO.ap[:, 0:128], in1=den.ap, op=ALU.mult),
                        reads=[pO, den], writes=[OA])
                kb.dma("sp", [(oT[hd * 128:(hd + 1) * 128, :], OA.ap)], reads=[OA])
            for hh in range(3):
                headA(g * 3 + hh)
            kb.barrier()
        for g in range(NKV):
            groupA(g)
        UT = kb.alloc([8, T], BF16, dma=True)
        kb.dma("sp", [(UT.ap, projT[R_U:R_U + 1024, :].rearrange("(c p) t -> p c t", p=128))], writes=[UT])
        Ut = kb.alloc([16, 8, 128], BF16)
        PW = kb.alloc([4, 2, 256], BF16, dma=True)
        kb.dma("pool", [(PW.ap, pool_w[l].rearrange("g (cc p) d -> p g cc d", p=128))], writes=[PW])
        for sc in range(16):
            f = lambda i, sc=sc: UT.ap[:, i, sc * 128:(sc + 1) * 128]
            f.bufs = [UT]
            transpose_to(Ut, f, 8, PS[6 + sc % 2], dst_ap=Ut.ap[:, sc, :, :])
        MT = [kb.alloc([6, 2, 512], BF16, dma=True) for _ in range(2)]
        ZB = [kb.alloc([512], BF16) for _ in range(4)]
        OB = [kb.alloc([512], BF16, dma=True) for _ in range(4)]
        it = 0
        for g in range(4):
            for i in range(4):
                mt = MT[it % 2]
                kb.dma("act", [(mt.ap, mt_in[g, i].rearrange("p (k a t) -> p k a t", k=6, a=2))], writes=[mt])
                ks = [k for k in range(6) if 0 <= 4 * i - 1 + k < 16]
                for cc in range(2):
                    pz = PS[(it * 2 + cc) % 4]
                    zb = ZB[(it * 2 + cc) % 4]
                    seq = [(k, a) for k in ks for a in range(2)]
                    for (k, a) in seq:
                        sidx = 4 * i - 1 + k
                        kb.mm(lambda e, pz=pz, sidx=sidx, g=g, cc=cc, k=k, a=a, mt=mt, seq=seq: e.matmul(
                            pz.ap, Ut.ap[:, sidx, 2 * g + cc, :], mt.ap[:, k, a, :],
                            start=((k, a) == seq[0]), stop=((k, a) == seq[-1])),
                            reads=[Ut, mt], writes=[pz], inc=((k, a) == seq[-1]))
                    kb.op("act", lambda e, pz=pz, zb=zb: e.activation(out=zb.ap, in_=pz.ap, func=AF.Copy),
                          reads=[pz], writes=[zb])
                for dd in range(2):
                    po = PS[4 + (it * 2 + dd) % 4]
                    ob = OB[(it * 2 + dd) % 4]
                    for cc in range(2):
                        zb = ZB[(it * 2 + cc) % 4]
                        kb.mm(lambda e, po=po, g=g, cc=cc, dd=dd, zb=zb: e.matmul(
                            po.ap, PW.ap[:, g, cc, dd * 128:(dd + 1) * 128], zb.ap, start=(cc == 0), stop=(cc == 1)),
                            reads=[PW, zb], writes=[po], inc=(cc == 1))
                    kb.op("act", lambda e, po=po, ob=ob, g=g, dd=dd: e.activation(
                        out=ob.ap, in_=po.ap, func=AF.Identity, scale=v(l, V_PS + 2 * g + dd)),
                        reads=[po, VEC], writes=[ob])
                    row = 1536 + (2 * g + dd) * 128
                    kb.dma("sp", [(oT[row:row + 128, i * 512:(i + 1) * 512], ob.ap)], reads=[ob])
                it += 1
        kb.barrier()
        def headC(hd, DIST, NBH):
            slope = 2.0 ** (-8.0 * (hd + 1) / NCH)
            kb.op("dve", lambda e, slope=slope: e.tensor_scalar(out=NBH.ap, in0=DIST.ap, scalar1=-slope, scalar2=-150.0,
                                                                op0=ALU.mult, op1=ALU.max), reads=[DIST], writes=[NBH])
            QT = kb.alloc([T], BF16, dma=True)
            KT = kb.alloc([T], BF16, dma=True)
            VT = kb.alloc([T], BF16, dma=True)
            Vt = kb.alloc([16, 128], BF16)
            OC = kb.alloc([T], BF16, dma=True)
            kb.dma("sp", [(QT.ap, projT[R_QC + hd * 128:R_QC + (hd + 1) * 128, :])], writes=[QT])
            kb.dma("sp", [(KT.ap, projT[R_KC + hd * 128:R_KC + (hd + 1) * 128, :])], writes=[KT])
            kb.dma("sp", [(VT.ap, projT[R_VC + hd * 128:R_VC + (hd + 1) * 128, :])], writes=[VT])
            f = lambda i: VT.ap[:, i * 128:(i + 1) * 128]
            f.bufs = [VT]
            transpose_to(Vt, f, 16, PS[7])
            EIN = [kb.alloc([512], F32) for _ in range(3)]
            E = [[kb.alloc([512], BF16) for _ in range(16)] for _ in range(2)]
            R12 = [kb.alloc([512], F32) for _ in range(2)]
            T12 = [kb.alloc([512], F32) for _ in range(2)]
            SQo = kb.alloc([512], BF16)
            RSo = kb.alloc([512], F32)
            cnt = 0
            for qt in range(4):
                q0 = qt * 512
                for c in range(2):
                    for j in range(16):
                        pS = PS[cnt % 3]
                        ein = EIN[cnt % 3]
                        cnt += 1
                        kb.mm(lambda e, pS=pS, c=c, j=j, q0=q0: e.matmul(
                            pS.ap, KT.ap[c * 64:(c + 1) * 64, j * 128:(j + 1) * 128],
                            QT.ap[c * 64:(c + 1) * 64, q0:q0 + 512], start=True, stop=True),
                            reads=[KT, QT], writes=[pS], inc=True)
                        d0 = q0 - j * 128 + DOFF
                        kb.op("dve", lambda e, pS=pS, ein=ein, d0=d0: e.scalar_tensor_tensor(
                            out=ein.ap, in0=pS.ap, scalar=scaleC, in1=NBH.ap[:, d0:d0 + 512],
                            op0=ALU.mult, op1=ALU.add), reads=[pS, NBH], writes=[ein])
                        kb.op("act", lambda e, ein=ein, ee=E[c][j]: e.activation(out=ee.ap, in_=ein.ap, func=AF.Exp),
                              reads=[ein], writes=[E[c][j]])
                pO = [PS[3], PS[4]]
                pM = [PS[5], PS[6]]
                for c in range(2):
                    for j in range(16):
                        kb.mm(lambda e, c=c, j=j: e.matmul(pO[c].ap, Vt.ap[:, j, :], E[c][j].ap,
                                                          start=(j == 0), stop=(j == 15)),
                              reads=[Vt, E[c][j]], writes=[pO[c]])
                    for j in range(16):
                        kb.mm(lambda e, c=c, j=j: e.matmul(pM[c].ap, ONES_1.ap, E[c][j].ap,
                                                          start=(j == 0), stop=(j == 15)),
                              reads=[ONES_1, E[c][j]], writes=[pM[c]], inc=(j == 15))
                for c in range(2):
                    kb.op("dve", lambda e, c=c: e.reciprocal(out=R12[c].ap, in_=pM[c].ap), reads=[pM[c]], writes=[R12[c]])
                    kb.op("dve", lambda e, c=c: e.tensor_tensor(out=T12[c].ap, in0=pO[c].ap, in1=R12[c].ap, op=ALU.mult),
                          reads=[pO[c], R12[c]], writes=[T12[c]])
                kb.op("dve", lambda e: e.scalar_tensor_tensor(out=T12[0].ap, in0=T12[1].ap, scalar=dr.ap[:, 148:149],
                                                              in1=T12[0].ap, op0=ALU.mult, op1=ALU.add),
                      reads=[T12[1], T12[0], dr], writes=[T12[0]])
                kb.op("act", lambda e: e.activation(out=SQo.ap, in_=T12[0].ap, func=AF.Square), reads=[T12[0]], writes=[SQo])
                kb.mm(lambda e: e.matmul(PS[7].ap, ONES_H.ap, SQo.ap, start=True, stop=True),
                      reads=[ONES_H, SQo], writes=[PS[7]], inc=True)
                kb.op("act", lambda e: e.activation(out=RSo.ap, in_=PS[7].ap, func=AF.Sqrt, bias=EPS.ap[:, 1:2], scale=1.0),
                      reads=[PS[7], EPS], writes=[RSo])
                kb.op("dve", lambda e: e.reciprocal(out=RSo.ap, in_=RSo.ap), reads=[RSo], writes=[RSo])
                kb.op("dve", lambda e: e.tensor_tensor(out=T12[1].ap, in0=T12[0].ap, in1=RSo.ap, op=ALU.mult),
                      reads=[T12[0], RSo], writes=[T12[1]])
                kb.op("act", lambda e, q0=q0: e.activation(out=OC.ap[:, q0:q0 + 512], in_=T12[1].ap, func=AF.Identity,
                                                          scale=dr.ap[:, 149:150]),
                      reads=[T12[1], dr], writes=[OC])
            kb.dma("sp", [(oT[2560 + hd * 128:2560 + (hd + 1) * 128, :], OC.ap)], reads=[OC])
        for hp in range(NCH // 2):
            DIST = kb.alloc([DW], F32, dma=True)
            kb.dma("sp", [(DIST.ap, dist_in)], writes=[DIST])
            NBH = kb.alloc([DW], F32)
            headC(2 * hp, DIST, NBH)
            headC(2 * hp + 1, DIST, NBH)
            kb.barrier()

    for l in range(depth):
        if l == 0:
            norm_stage(None, 0, None, xin, None, 0, 1)
        for tt in range(2):
            gemm_proj_gates(l, tt)
        mixer_stage(l)
        for tt in range(2):
            gemm_branch(l, tt)
        for tt in range(2):
            gemm_plain_f32(lambda m: W["w_out"][l][:, m * 256:(m + 1) * 256], 16,
                           mergedT[:, tt * 1024:(tt + 1) * 1024], KC, yT, tt * 1024)
        norm_stage(l, 1, yT, xin if l == 0 else xres, xres, l, 2)
        for tt in range(2):
            gemm_ffn_up(l, tt)
        for tt in range(2):
            gemm_plain_f32(lambda m: W["ffn_w_down"][l][0:5504, m * 256:(m + 1) * 256], 16,
                           sT[0:5504, tt * 1024:(tt + 1) * 1024], 43, y1T, tt * 1024)
            gemm_plain_f32(lambda m: W["ffn_w_down"][l][5504:FF, m * 256:(m + 1) * 256], 16,
                           sT[5504:FF, tt * 1024:(tt + 1) * 1024], 43, yT, tt * 1024, add_src=y1T)
        last = (l == depth - 1)
        norm_stage(l, 2, yT, xres, out if last else xres, None if last else l + 1, 0 if last else 1)

    with nc.Block() as block:
        def run(name):
            def f(eng):
                for fn in kb.q[name]:
                    fn(eng)
            return f
        block.sync(run("sp"))
        block.scalar(run("act"))
        block.tensor(run("pe"))
        block.vector(run("dve"))
        block.gpsimd(run("pool"))
    return nc


def _consts():
    bf = ml_dtypes.bfloat16
    ident = np.eye(128, dtype=np.float32).astype(bf)
    i = np.arange(128)[:, None]
    m = np.arange(DW)[None, :]
    dist = np.abs(m - i - DOFF).astype(np.float32)
    slopes = 2.0 ** (-8.0 * np.arange(1, NA + 1) / NA)
    c = np.arange(384)[None, :]
    rel = (1 - c // 128) * 128 + (c % 128) - i
    ba = np.where((np.abs(rel) <= 128)[:, None, :], -slopes[None, :, None] * np.abs(rel)[:, None, :].astype(np.float64), -300.0)
    biasA = ba.astype(np.float32).reshape(128, NA * 384)
    mt = np.zeros((4, 4, 128, 6, 2, 512), dtype=bf)
    pos = np.arange(T)
    for g, w in enumerate(POOL_WINDOWS):
        r = w // 2
        lo = np.maximum(pos - r, 0)
        hi = np.minimum(pos + r + 1, T)
        cnt = (hi - lo).astype(np.float64)
        M = np.zeros((T, T), dtype=np.float64)
        for t in range(T):
            M[t, lo[t]:hi[t]] = 1.0 / cnt[t]
            M[t, t] -= 1.0
        M32 = M.astype(np.float32)
        Mh = M32.astype(bf)
        Ml = (M32 - Mh.astype(np.float32)).astype(bf)
        for it in range(4):
            for k in range(6):
                sidx = 4 * it - 1 + k
                if not (0 <= sidx < 16):
                    continue
                rows = slice(sidx * 128, (sidx + 1) * 128)
                colst = slice(it * 512, (it + 1) * 512)
                mt[g, it, :, k, 0, :] = Mh[colst, rows].T
                mt[g, it, :, k, 1, :] = Ml[colst, rows].T
    return ident, dist, biasA, mt.reshape(4, 4, 128, 6 * 2 * 512)


def _pc(vv):
    return np.ascontiguousarray(np.asarray(vv, dtype=np.float32).reshape(-1, 128).T)


def _make_in_maps(inputs, depth, cores):
    ident, dist, biasA, mt = _consts()
    shared = {"ident": ident, "dist": dist, "biasA": biasA, "mt": mt}
    for l in range(depth):
        for nm in ("ada_w", "w_in", "w_gate", "w_branch", "w_out", "ffn_w_gate", "ffn_w_up", "ffn_w_down"):
            shared[f"{nm}{l}"] = np.ascontiguousarray(inputs[nm][l])
        shared[f"pool_w{l}"] = np.ascontiguousarray(inputs["pool_w"][l])
    in_maps = []
    for b in cores:
        vec = np.zeros((128, NV), dtype=np.float32)
        for l in range(depth):
            o = l * VL
            vec[:, o + V_PRE1:o + V_PRE1 + 32] = _pc(inputs["mix_pre_g"][l])
            vec[:, o + V_POST1:o + V_POST1 + 32] = _pc(inputs["mix_post_g"][l])
            vec[:, o + V_PRE2:o + V_PRE2 + 32] = _pc(inputs["ffn_pre_g"][l])
            vec[:, o + V_POST2:o + V_POST2 + 32] = _pc(inputs["ffn_post_g"][l])
            vec[:, o + V_BG:o + V_BG + 96] = _pc(inputs["b_gate"][l])
            vec[:, o + V_PS:o + V_PS + 8] = _pc(inputs["pool_scale"][l])
            vec[:, o + V_AB:o + V_AB + 192] = _pc(inputs["ada_b"][l])
            vec[:, o + V_SUB:o + V_SUB + 1] = np.asarray(inputs["diff_subln_g"][l], dtype=np.float32).reshape(128, 1)
            vec[:, o + V_SINK:o + V_SINK + 12] = np.asarray(inputs["attn_sink"][l], dtype=np.float32)[None, :]
            vec[:, o + V_DL:o + V_DL + 256] = np.asarray(inputs["diff_lambda"][l], dtype=np.float32).reshape(1, 256)
        vec[:, V_C:V_C + 32] = _pc(inputs["c"][b])
        m = dict(shared)
        m["vecs"] = vec
        m["xT"] = np.ascontiguousarray(np.asarray(inputs["x"][b], dtype=np.float32).T)
        in_maps.append(m)
    return in_maps


def kernel(**inputs):
    nc = build_program(DEPTH)
    in_maps = _make_in_maps(inputs, DEPTH, list(range(NCORES)))
    res = run_bass_kernel_spmd(nc, in_maps, core_ids=list(range(NCORES)))
    outs = [np.ascontiguousarray(np.asarray(r["out"]).T) for r in res.results]
    return np.stack(outs, axis=0).astype(np.float32)
```

```python
import math
import numpy as np
import ml_dtypes
import concourse.bass as bass
import concourse.mybir as mybir
from concourse.bass_utils import run_bass_kernel_spmd

F32 = mybir.dt.float32
BF16 = mybir.dt.bfloat16
AF = mybir.ActivationFunctionType
ALU = mybir.AluOpType

D = 4096
T = 2048
KC = 32
FF = 11008
FC = 86
NCORES = 8
DEPTH = 2
HD = 128
NA = 12
NKV = 4
NCH = 12
R_QA, R_KA, R_VA, R_U, R_QC, R_KC, R_VC = 0, 1536, 2048, 2560, 3584, 5120, 6656
POOL_WINDOWS = (2, 4, 8, 16)
DOFF = 1920
DW = 4352
ENG = ("sp", "act", "pe", "dve", "pool")
PROFILE_SCOPES = False

V_PRE1, V_POST1, V_PRE2, V_POST2, V_BG, V_PS, V_AB, V_SUB, V_SINK, V_DL = 0, 32, 64, 96, 128, 224, 232, 424, 425, 437
VL = 437 + 256
V_C = 2 * VL
NV = V_C + 32


class Sem:
    def __init__(self, nc, name):
        self.h = nc.alloc_semaphore(name)
        self.v = 0


class Buf:
    def __init__(self, ap, dsem=None):
        self.ap = ap
        self.wr = None
        self.rd = {}
        self.dsem = dsem


class KB:
    def __init__(self, nc):
        self.nc = nc
        self.q = {e: [] for e in ENG}
        self.waited = {e: {} for e in ENG}
        self.sems = []
        self.esem = {e: self.sem("e_" + e) for e in ("act", "pe", "dve", "pool")}
        self.pe_pending = []
        self.dpool = [self.sem(f"d{i}") for i in range(40)]
        self.dnext = 0
        self.arena_t = nc.alloc_sbuf_tensor("arena", [128, 176 * 256], F32)
        self.aoff = 0

    def sem(self, name):
        s = Sem(self.nc, name)
        self.sems.append(s)
        return s

    def dsem(self):
        s = self.dpool[self.dnext % len(self.dpool)]
        self.dnext += 1
        return s

    def alloc(self, free_shape, dt, dma=False):
        n = int(np.prod(free_shape))
        nb = n * (4 if dt == F32 else 2)
        n32 = (nb + 31) // 32 * 8
        assert self.aoff + n32 <= 176 * 256, ("arena overflow", self.aoff, n32)
        ap = self.arena_t[:, self.aoff:self.aoff + n32]
        self.aoff += n32
        if dt != F32:
            ap = ap.bitcast(dt)
        ap = ap[:, :n]
        if len(free_shape) == 2:
            ap = ap.rearrange("p (a b) -> p a b", a=free_shape[0])
        elif len(free_shape) == 3:
            ap = ap.rearrange("p (a b c) -> p a b c", a=free_shape[0], b=free_shape[1])
        elif len(free_shape) == 4:
            ap = ap.rearrange("p (a b c d) -> p a b c d", a=free_shape[0], b=free_shape[1], c=free_shape[2])
        return Buf(ap, self.dsem() if dma else None)

    def wait(self, eng, tok):
        if tok is None:
            return
        s, v = tok
        w = self.waited[eng]
        if w.get(s, 0) >= v:
            return
        w[s] = v
        self.q[eng].append(lambda e: e.wait_ge(s.h, v))

    def _deps(self, eng, reads, writes, extra):
        for b in reads:
            self.wait(eng, b.wr)
        for b in writes:
            self.wait(eng, b.wr)
            for s_, v_ in b.rd.items():
                self.wait(eng, (s_, v_))
        for t in extra:
            self.wait(eng, t)

    def _commit(self, tok, reads, writes):
        for b in reads:
            b.rd[tok[0]] = max(b.rd.get(tok[0], 0), tok[1])
        for b in writes:
            b.wr = tok
            b.rd = {}

    def op(self, eng, fn, reads=(), writes=(), extra=()):
        self._deps(eng, reads, writes, extra)
        sem = self.esem[eng]
        sem.v += 1
        v = sem.v
        self.q[eng].append(lambda e: fn(e).then_inc(sem.h, 1))
        tok = (sem, v)
        self._commit(tok, reads, writes)
        return tok

    def mm(self, fn, reads=(), writes=(), inc=False, extra=()):
        self._deps("pe", reads, writes, extra)
        self.pe_pending.append((list(reads), list(writes)))
        if not inc:
            self.q["pe"].append(fn)
            return None
        sem = self.esem["pe"]
        sem.v += 1
        v = sem.v
        self.q["pe"].append(lambda e: fn(e).then_inc(sem.h, 1))
        tok = (sem, v)
        for r, w in self.pe_pending:
            self._commit(tok, r, w)
        self.pe_pending = []
        return tok

    def dma(self, eng, pairs, reads=(), writes=(), sem=None, extra=()):
        self._deps(eng, reads, writes, extra)
        if sem is None:
            sem = (list(writes) + list(reads))[0].dsem
        for (o, i) in pairs:
            sem.v += 16
            self.q[eng].append(lambda e, o=o, i=i: e.dma_start(out=o, in_=i).then_inc(sem.h, 16))
        tok = (sem, sem.v)
        self._commit(tok, reads, writes)
        return tok

    def scope(self, name):
        for e in ENG:
            self.q[e].append(("scope", name))

    def barrier(self):
        assert not self.pe_pending
        toks = [(s, s.v) for s in self.sems if s.v > 0]
        for e in ENG:
            for t in toks:
                self.wait(e, t)
        self.aoff = 0


def build_program(depth=DEPTH, debug=False):
    nc = bass.Bass("TRN2", target_bir_lowering=False)
    kb = KB(nc)

    def din(name, shape, dt):
        return nc.dram_tensor(name, list(shape), dt, kind="ExternalInput").ap()

    def dscr(name, shape, dt):
        kind = "ExternalOutput" if debug else "Internal"
        return nc.dram_tensor(name, list(shape), dt, kind=kind).ap()

    xin = din("xT", [D, T], F32)
    vecs = din("vecs", [128, NV], F32)
    ident_in = din("ident", [128, 128], BF16)
    dist_in = din("dist", [128, DW], F32)
    biasA_in = din("biasA", [128, NA * 384], F32)
    mt_in = din("mt", [4, 4, 128, 6 * 2 * 512], BF16)
    W = {}
    for nm, shp in (("ada_w", [D, 6 * D]), ("w_in", [D, 8192]), ("w_gate", [D, 3 * D]), ("w_branch", [D, D]),
                    ("w_out", [D, D]), ("ffn_w_gate", [D, FF]), ("ffn_w_up", [D, FF]), ("ffn_w_down", [FF, D])):
        W[nm] = [din(f"{nm}{l}", shp, F32) for l in range(depth)]
    pool_w = [din(f"pool_w{l}", [4, 256, 256], F32) for l in range(depth)]
    out = nc.dram_tensor("out", [D, T], F32, kind="ExternalOutput").ap()

    xres = dscr("xres", [D, T], F32)
    hT = dscr("hT", [D, T], BF16)
    projT = dscr("projT", [8192, T], BF16)
    gatesT = dscr("gatesT", [3 * D, T], BF16)
    oT = dscr("oT", [D, T], BF16)
    mergedT = dscr("mergedT", [D, T], BF16)
    yT = dscr("yT", [D, T], F32)
    y1T = dscr("y1T", [D, T], F32)
    sT = dscr("sT", [FF, T], BF16)

    def pers(name, shape, dt):
        return Buf(nc.alloc_sbuf_tensor(name, shape, dt)[:], None)

    VEC = pers("VEC", [128, NV], F32)
    VEC.dsem = kb.sem("vecd")
    IDENT = pers("IDENT", [128, 128], BF16)
    IDENT.dsem = kb.sem("identd")
    ONES_D = pers("ONES_D", [128, 128], BF16)
    ONES_H = pers("ONES_H", [128, 128], BF16)
    ONES_1 = pers("ONES_1", [128, 128], BF16)
    EPS = pers("EPS", [128, 2], F32)
    MOD = [pers(f"MOD{l}", [128, 192], F32) for l in range(depth)]
    DER = [pers(f"DER{l}", [128, 4 * 32 + 16 + 8], F32) for l in range(depth)]
    COND = pers("COND", [128, KC], BF16)
    SCR = pers("SCR", [128, 64], F32)
    PS = [Buf(nc.alloc_psum_tensor(f"ps{i}", [128, 512], F32)[:], None) for i in range(8)]

    def v(l, off, n=1):
        return VEC.ap[:, l * VL + off: l * VL + off + n]

    kb.dma("sp", [(VEC.ap, vecs)], writes=[VEC])
    kb.dma("sp", [(IDENT.ap, ident_in)], writes=[IDENT])
    kb.op("pool", lambda e: e.memset(ONES_D.ap, 1.0 / D), writes=[ONES_D])
    kb.op("pool", lambda e: e.memset(ONES_H.ap, 1.0 / HD), writes=[ONES_H])
    kb.op("pool", lambda e: e.memset(ONES_1.ap, 1.0), writes=[ONES_1])
    kb.op("pool", lambda e: e.memset(EPS.ap[:, 0:1], 1e-6), writes=[EPS])
    kb.op("pool", lambda e: e.memset(EPS.ap[:, 1:2], 1e-5), writes=[EPS])
    kb.op("act", lambda e: e.activation(out=COND.ap, in_=VEC.ap[:, V_C:V_C + 32], func=AF.Silu),
          reads=[VEC], writes=[COND])
    kb.barrier()

    def gemm(a_ap, kcn, tw, wblocks, epi, ksplits=None, setup=None, side=None):
        nh = max(1, tw // 512)
        hw = min(tw, 512)
        if ksplits is None:
            ksplits = [0, kcn]
        ng = len(ksplits) - 1
        q = (kcn + 3) // 4
        kq = [(k0, min(k0 + q, kcn)) for k0 in range(0, kcn, q)]

        def part(kc):
            return kc // q
        if tw > 1:
            Afull = kb.alloc([kcn, tw], BF16)
            Aap = Afull.ap
            av = a_ap.rearrange("(kc p) t -> p kc t", p=128)
            AP_ = []
            for (k0, k1) in kq:
                bb = Buf(Aap[:, k0:k1, :], kb.dsem())
                kb.dma("sp", [(bb.ap, av[:, k0:k1, :])], writes=[bb])
                AP_.append(bb)
        else:
            Aap = a_ap.ap
            AP_ = [a_ap] * 4
        WB = []
        for _ in range(3):
            wfull = kb.alloc([kcn, 256], BF16)
            WB.append((wfull.ap, [Buf(wfull.ap[:, k0:k1, :], kb.dsem()) for (k0, k1) in kq]))
        ctx = setup() if setup else None
        bank_i = 0
        side_i = 0
        if side:
            SW = [kb.alloc([KC, 256], BF16, dma=True) for _ in range(2)]

        def do_side():
            nonlocal side_i, bank_i
            sw_ap, smeta = side["blocks"][side_i]
            sb = SW[side_i % 2]
            side_i += 1
            swv = sw_ap.rearrange("(kc p) n -> p kc n", p=128)
            kb.dma("pool", [(sb.ap[:, k0:k0 + 8, :], swv[:, k0:k0 + 8, :]) for k0 in range(0, KC, 8)], writes=[sb])
            for j in range(2):
                pb = PS[bank_i % 8]
                bank_i += 1
                for kc in range(KC):
                    kb.mm(lambda e, pb=pb, sb=sb, kc=kc, j=j: e.matmul(
                        pb.ap[:, 0:1], sb.ap[:, kc, j * 128:(j + 1) * 128], COND.ap[:, kc:kc + 1],
                        start=(kc == 0), stop=(kc == KC - 1)), reads=[sb, COND], writes=[pb], inc=(kc == KC - 1))
                side["epi"](None, smeta, j, 0, [pb])
        for b, (w_ap, meta) in enumerate(wblocks):
            wap, wparts = WB[b % 3]
            wv = w_ap.rearrange("(kc p) n -> p kc n", p=128)
            for pi, (k0, k1) in enumerate(kq):
                kb.dma("pool", [(wparts[pi].ap, wv[:, k0:k1, :])], writes=[wparts[pi]])
            for j in range(2):
                banks = []
                for h in range(nh):
                    bl = []
                    for g in range(ng):
                        bl.append(PS[bank_i % 8])
                        bank_i += 1
                    banks.append(bl)
                toks = [None] * nh
                for g in range(ng):
                    k0, k1 = ksplits[g], ksplits[g + 1]
                    for kc in range(k0, k1):
                        for h in range(nh):
                            last = (kc == k1 - 1)
                            fin = last and (g == ng - 1)
                            pb = banks[h][g]
                            t = kb.mm(lambda e, pb=pb, wap=wap, kc=kc, j=j, h=h, k0=k0, last=last:
                                      e.matmul(pb.ap[:, 0:hw], wap[:, kc, j * 128:(j + 1) * 128],
                                               Aap[:, kc, h * hw:(h + 1) * hw] if tw > 1 else Aap[:, kc:kc + 1],
                                               start=(kc == k0), stop=last),
                                      reads=[wparts[part(kc)], AP_[part(kc)]], writes=[pb], inc=fin)
                            if fin:
                                toks[h] = t
                for h in range(nh):
                    epi(ctx, meta, j, h, banks[h])
            if side and b % side["every"] == side["every"] - 1 and side_i < len(side["blocks"]):
                do_side()
        while side and side_i < len(side["blocks"]):
            do_side()

    def make_ring(n, shape, dt):
        return {"b": [kb.alloc(shape, dt, dma=True) for _ in range(n)], "i": 0}

    def ring_next(r):
        b = r["b"][r["i"] % len(r["b"])]
        r["i"] += 1
        return b

    def ada_epi(l):
        def epi_ada(ctx, meta, j, h, banks):
            col = meta * 2 + j
            kb.op("act", lambda e: e.activation(out=MOD[l].ap[:, col:col + 1], in_=banks[0].ap[:, 0:1],
                                                func=AF.Identity, bias=v(l, V_AB + col), scale=1.0),
                  reads=[banks[0], VEC], writes=[MOD[l]])
        return epi_ada

    def ada_blocks(l, m0, m1):
        return [(W["ada_w"][l][:, m * 256:(m + 1) * 256], m) for m in range(m0, m1)]

    def derive_pre1(l):
        dr = DER[l]
        kb.op("dve", lambda e: e.scalar_tensor_tensor(
            out=dr.ap[:, 0:32], in0=MOD[l].ap[:, 32:64], scalar=1.0, in1=v(l, V_PRE1, 32),
            op0=ALU.add, op1=ALU.mult), reads=[MOD[l], VEC], writes=[dr])

    def derive_rest(l):
        dr = DER[l]
        kb.op("dve", lambda e: e.scalar_tensor_tensor(
            out=dr.ap[:, 64:96], in0=MOD[l].ap[:, 128:160], scalar=1.0, in1=v(l, V_PRE2, 32),
            op0=ALU.add, op1=ALU.mult), reads=[MOD[l], VEC], writes=[dr])
        for (dst, gt, g) in ((32, 64, V_POST1), (96, 160, V_POST2)):
            kb.op("dve", lambda e, dst=dst, gt=gt, g=g: e.tensor_tensor(
                out=dr.ap[:, dst:dst + 32], in0=MOD[l].ap[:, gt:gt + 32], in1=v(l, g, 32), op=ALU.mult),
                reads=[MOD[l], VEC], writes=[dr])

    def derive_misc(l):
        dr = DER[l]
        kb.op("act", lambda e: e.activation(out=dr.ap[:, 128:140], in_=v(l, V_SINK, 12), func=AF.Exp),
              reads=[VEC], writes=[dr])
        lam_init = 0.8 - 0.6 * math.exp(-0.3 * l)
        kb.op("dve", lambda e: e.tensor_tensor(out=SCR.ap[:, 0:64], in0=v(l, V_DL, 64), in1=v(l, V_DL + 64, 64),
                                               op=ALU.mult), reads=[VEC], writes=[SCR])
        kb.op("dve", lambda e: e.reduce_sum(out=dr.ap[:, 144:145], in_=SCR.ap[:, 0:64], axis=mybir.AxisListType.X),
              reads=[SCR], writes=[dr])
        kb.op("dve", lambda e: e.tensor_tensor(out=SCR.ap[:, 0:64], in0=v(l, V_DL + 128, 64), in1=v(l, V_DL + 192, 64),
                                               op=ALU.mult), reads=[VEC, dr], writes=[SCR])
        kb.op("dve", lambda e: e.reduce_sum(out=dr.ap[:, 145:146], in_=SCR.ap[:, 0:64], axis=mybir.AxisListType.X),
              reads=[SCR], writes=[dr])
        kb.op("act", lambda e: e.activation(out=dr.ap[:, 146:148], in_=dr.ap[:, 144:146], func=AF.Exp),
              reads=[dr], writes=[dr])
        kb.op("dve", lambda e: e.tensor_tensor(out=dr.ap[:, 148:149], in0=dr.ap[:, 147:148], in1=dr.ap[:, 146:147],
                                               op=ALU.subtract), reads=[dr], writes=[dr])
        kb.op("dve", lambda e: e.tensor_scalar(out=dr.ap[:, 148:149], in0=dr.ap[:, 148:149], scalar1=-lam_init,
                                               scalar2=None, op0=ALU.add), reads=[dr], writes=[dr])
        kb.op("dve", lambda e: e.tensor_scalar(out=dr.ap[:, 149:150], in0=v(l, V_SUB), scalar1=1.0 - lam_init,
                                               scalar2=None, op0=ALU.mult), reads=[VEC, dr], writes=[dr])

    for l in range(depth):
        derive_misc(l)
    gemm(COND, KC, 1, ada_blocks(0, 0, 32), ada_epi(0))
    kb.barrier()
    derive_pre1(0)
    kb.barrier()

    def norm_stage(l_post, post, y_src, x_src, x_dst, l_pre, pre):
        NT = 256
        sets = []
        for _ in range(2):
            st = {"X": kb.alloc([KC, NT], F32, dma=True), "SQ": [kb.alloc([NT], BF16) for _ in range(2)],
                  "TMP": [kb.alloc([NT], F32) for _ in range(2)], "RS": kb.alloc([NT], F32)}
            if post:
                st["Y"] = kb.alloc([KC, NT], F32, dma=True)
            if pre:
                st["H"] = kb.alloc([KC, NT], BF16, dma=True)
            sets.append(st)
        xv = x_src.rearrange("(kc p) t -> p kc t", p=128)

        def tile(tt):
            st = sets[tt % 2]
            ts = slice(tt * NT, (tt + 1) * NT)
            X, SQ, TMP, RS = st["X"], st["SQ"], st["TMP"], st["RS"]
            kb.dma("pool", [(X.ap[:, k0:k0 + 8, :], xv[:, k0:k0 + 8, ts]) for k0 in range(0, KC, 8)], writes=[X])
            pss = PS[tt % 2]

            def rstd_of(SRC):
                for kc in range(KC):
                    sq = SQ[kc % 2]
                    kb.op("act", lambda e, sq=sq, kc=kc: e.activation(out=sq.ap, in_=SRC.ap[:, kc, :], func=AF.Square),
                          reads=[SRC], writes=[sq])
                    kb.mm(lambda e, sq=sq, kc=kc: e.matmul(pss.ap[:, 0:NT], ONES_D.ap, sq.ap, start=(kc == 0), stop=(kc == KC - 1)),
                          reads=[sq, ONES_D], writes=[pss], inc=True)
                kb.op("act", lambda e: e.activation(out=RS.ap, in_=pss.ap[:, 0:NT], func=AF.Sqrt, bias=EPS.ap[:, 0:1], scale=1.0),
                      reads=[pss, EPS], writes=[RS])
                kb.op("dve", lambda e: e.reciprocal(out=RS.ap, in_=RS.ap), reads=[RS], writes=[RS])

            if post:
                Y = st["Y"]
                yv = y_src.rearrange("(kc p) t -> p kc t", p=128)
                kb.dma("pool", [(Y.ap[:, k0:k0 + 8, :], yv[:, k0:k0 + 8, ts]) for k0 in range(0, KC, 8)], writes=[Y])
                rstd_of(Y)
                gp = DER[l_post].ap[:, (32 if post == 1 else 96):]
                for kc in range(KC):
                    tm = TMP[kc % 2]
                    kb.op("dve", lambda e, tm=tm, kc=kc: e.tensor_tensor(out=tm.ap, in0=Y.ap[:, kc, :], in1=RS.ap, op=ALU.mult),
                          reads=[Y, RS], writes=[tm])
                    kb.op("dve", lambda e, tm=tm, kc=kc: e.scalar_tensor_tensor(
                        out=X.ap[:, kc, :], in0=tm.ap, scalar=gp[:, kc:kc + 1], in1=X.ap[:, kc, :],
                        op0=ALU.mult, op1=ALU.add), reads=[tm, DER[l_post], X], writes=[X])
                xo = x_dst.rearrange("(kc p) t -> p kc t", p=128)
                kb.dma("sp", [(xo[:, k0:k0 + 8, ts], X.ap[:, k0:k0 + 8, :]) for k0 in range(0, KC, 8)], reads=[X])
            if pre:
                H = st["H"]
                rstd_of(X)
                go = 0 if pre == 1 else 64
                sh = MOD[l_pre].ap[:, (0 if pre == 1 else 96):]
                g_ = DER[l_pre].ap[:, go:]
                for kc in range(KC):
                    tm = TMP[kc % 2]
                    kb.op("dve", lambda e, tm=tm, kc=kc: e.tensor_tensor(out=tm.ap, in0=X.ap[:, kc, :], in1=RS.ap, op=ALU.mult),
                          reads=[X, RS], writes=[tm])
                    kb.op("act", lambda e, tm=tm, kc=kc: e.activation(out=H.ap[:, kc, :], in_=tm.ap, func=AF.Identity,
                                                                    scale=g_[:, kc:kc + 1], bias=sh[:, kc:kc + 1]),
                          reads=[tm, DER[l_pre], MOD[l_pre]], writes=[H])
                hv = hT.rearrange("(kc p) t -> p kc t", p=128)
                kb.dma("sp", [(hv[:, k0:k0 + 8, ts], H.ap[:, k0:k0 + 8, :]) for k0 in range(0, KC, 8)], reads=[H])
        for tt in range(T // NT):
            tile(tt)
        kb.barrier()

    def store(dst_ap, ob):
        kb.dma("sp", [(dst_ap, ob.ap)], reads=[ob])

    def gemm_proj_gates(l, tt, side=None):
        t0 = tt * 1024

        def setup():
            return make_ring(4, [512], BF16)

        def epi(ring, meta, j, h, banks):
            kind, m = meta
            ob = ring_next(ring)
            row = (m * 2 + j) * 128
            cols = slice(t0 + h * 512, t0 + (h + 1) * 512)
            if kind == "p":
                kb.op("act", lambda e: e.activation(out=ob.ap, in_=banks[0].ap, func=AF.Copy),
                      reads=[banks[0]], writes=[ob])
                store(projT[row:row + 128, cols], ob)
            else:
                bcol = V_BG + m * 2 + j
                kb.op("act", lambda e: e.activation(out=ob.ap, in_=banks[0].ap, func=AF.Sigmoid,
                                                    bias=v(l, bcol), scale=1.0),
                      reads=[banks[0], VEC], writes=[ob])
                store(gatesT[row:row + 128, cols], ob)
        blocks = [(W["w_in"][l][:, m * 256:(m + 1) * 256], ("p", m)) for m in range(32)]
        blocks += [(W["w_gate"][l][:, m * 256:(m + 1) * 256], ("g", m)) for m in range(48)]
        gemm(hT[:, t0:t0 + 1024], KC, 1024, blocks, epi, setup=setup, side=side)
        kb.barrier()

    def gemm_branch(l, tt):
        t0 = tt * 1024

        def setup():
            return (make_ring(4, [512], BF16), make_ring(2, [3, 512], BF16), [kb.alloc([512], F32) for _ in range(2)])

        gv = gatesT.rearrange("(i r) t -> r i t", i=3)

        def epi(ctx, meta, j, h, banks):
            ring, gring, tmp = ctx
            ob = ring_next(ring)
            G = ring_next(gring)
            row = (meta * 2 + j) * 128
            cols = slice(t0 + h * 512, t0 + (h + 1) * 512)
            kb.dma("act", [(G.ap, gv[row:row + 128, :, cols])], writes=[G])
            a, b_ = tmp
            kb.op("dve", lambda e: e.tensor_tensor(out=a.ap, in0=banks[0].ap, in1=G.ap[:, 0, :], op=ALU.mult),
                  reads=[banks[0], G], writes=[a])
            kb.op("dve", lambda e: e.tensor_tensor(out=b_.ap, in0=banks[1].ap, in1=G.ap[:, 1, :], op=ALU.mult),
                  reads=[banks[1], G], writes=[b_])
            kb.op("dve", lambda e: e.tensor_tensor(out=a.ap, in0=a.ap, in1=b_.ap, op=ALU.add),
                  reads=[a, b_], writes=[a])
            kb.op("dve", lambda e: e.tensor_tensor(out=b_.ap, in0=banks[2].ap, in1=G.ap[:, 2, :], op=ALU.mult),
                  reads=[banks[2], G], writes=[b_])
            kb.op("dve", lambda e: e.tensor_tensor(out=ob.ap, in0=a.ap, in1=b_.ap, op=ALU.add),
                  reads=[a, b_], writes=[ob])
            store(mergedT[row:row + 128, cols], ob)
        blocks = [(W["w_branch"][l][:, m * 256:(m + 1) * 256], m) for m in range(16)]
        gemm(oT[:, t0:t0 + 1024], KC, 1024, blocks, epi, ksplits=[0, 12, 20, 32], setup=setup)
        kb.barrier()

    def gemm_plain_f32(w_ap_fn, nblk, a_ap, kcn, dst, t0, add_src=None):
        def setup():
            return (make_ring(4, [512], F32), make_ring(2, [512], F32))

        def epi(ctx, meta, j, h, banks):
            ring, lring = ctx
            ob = ring_next(ring)
            row = (meta * 2 + j) * 128
            cols = slice(t0 + h * 512, t0 + (h + 1) * 512)
            if add_src is None:
                kb.op("act", lambda e: e.activation(out=ob.ap, in_=banks[0].ap, func=AF.Copy),
                      reads=[banks[0]], writes=[ob])
            else:
                L = ring_next(lring)
                kb.dma("act", [(L.ap, add_src[row:row + 128, cols])], writes=[L])
                kb.op("dve", lambda e: e.tensor_tensor(out=ob.ap, in0=banks[0].ap, in1=L.ap, op=ALU.add),
                      reads=[banks[0], L], writes=[ob])
            store(dst[row:row + 128, cols], ob)
        blocks = [(w_ap_fn(m), m) for m in range(nblk)]
        gemm(a_ap, kcn, 1024, blocks, epi, setup=setup)
        kb.barrier()

    def gemm_ffn_up(l, tt, side=None):
        t0 = tt * 1024

        def setup():
            return (make_ring(4, [512], BF16), [kb.alloc([512], F32) for _ in range(4)], {})

        def epi(ctx, meta, j, h, banks):
            ring, sa, st = ctx
            kind, m = meta
            cols = slice(t0 + h * 512, t0 + (h + 1) * 512)
            if kind == "a":
                s_ = sa[j * 2 + h]
                kb.op("act", lambda e: e.activation(out=s_.ap, in_=banks[0].ap, func=AF.Silu),
                      reads=[banks[0]], writes=[s_])
            else:
                s_ = sa[j * 2 + h]
                ob = ring_next(ring)
                row = (m * 2 + j) * 128
                kb.op("dve", lambda e: e.tensor_tensor(out=ob.ap, in0=banks[0].ap, in1=s_.ap, op=ALU.mult),
                      reads=[banks[0], s_], writes=[ob])
                store(sT[row:row + 128, cols], ob)
        blocks = []
        for m in range(43):
            blocks.append((W["ffn_w_gate"][l][:, m * 256:(m + 1) * 256], ("a", m)))
            blocks.append((W["ffn_w_up"][l][:, m * 256:(m + 1) * 256], ("b", m)))
        gemm(hT[:, t0:t0 + 1024], KC, 1024, blocks, epi, setup=setup, side=side)
        kb.barrier()

    def transpose_to(dst, src_ap_fn, n, psb, dst_ap=None):
        if dst_ap is None:
            dst_ap = dst.ap
        for i0 in range(0, n, 8):
            cnt = min(8, n - i0)
            pv = psb.ap.bitcast(BF16)
            for i in range(cnt):
                kb.mm(lambda e, i=i, i0=i0: e.transpose(pv[:, i * 128:(i + 1) * 128], src_ap_fn(i0 + i), IDENT.ap),
                      reads=[IDENT] + src_ap_fn.bufs, writes=[psb], inc=(i == cnt - 1))
            kb.op("dve", lambda e, i0=i0, cnt=cnt: e.tensor_copy(
                out=dst_ap[:, i0:i0 + cnt, :], in_=pv[:, 0:cnt * 128].rearrange("p (a b) -> p a b", a=cnt)),
                reads=[psb], writes=[dst])

    def mixer_stage(l):
        dr = DER[l]
        scaleA = HD ** -0.5
        scaleC = 64 ** -0.5
        kb.scope(f"attnA{l}")
        def groupA(g):
            BIAS = kb.alloc([NA, 384], F32, dma=True)
            kb.dma("sp", [(BIAS.ap, biasA_in.rearrange("p (h c) -> p h c", h=NA))], writes=[BIAS])
            KT = kb.alloc([T], BF16, dma=True)
            VT = kb.alloc([T], BF16, dma=True)
            Vt = kb.alloc([16, 128], BF16)
            kb.dma("sp", [(KT.ap, projT[R_KA + g * 128:R_KA + (g + 1) * 128, :])], writes=[KT])
            kb.dma("sp", [(VT.ap, projT[R_VA + g * 128:R_VA + (g + 1) * 128, :])], writes=[VT])
            f = lambda i: VT.ap[:, i * 128:(i + 1) * 128]
            f.bufs = [VT]
            transpose_to(Vt, f, 16, PS[7])
            def headA(hd):
                QT = kb.alloc([T], BF16, dma=True)
                OA = kb.alloc([T], BF16, dma=True)
                kb.dma("sp", [(QT.ap, projT[R_QA + hd * 128:R_QA + (hd + 1) * 128, :])], writes=[QT])
                EIN = [kb.alloc([384], F32) for _ in range(2)]
                EE = [kb.alloc([384], BF16) for _ in range(2)]
                DEN = [kb.alloc([128], F32) for _ in range(2)]
                for n in range(16):
                    slots = [s for s in range(3) if 0 <= n - 1 + s < 16]
                    c0, c1 = slots[0] * 128, (slots[-1] + 1) * 128
                    pS = PS[n % 2]
                    pO = PS[2 + n % 2]
                    ein, ee, den = EIN[n % 2], EE[n % 2], DEN[n % 2]
                    for s in slots:
                        jb = n - 1 + s
                        kb.mm(lambda e, s=s, jb=jb, n=n, pS=pS: e.matmul(
                            pS.ap[:, s * 128:(s + 1) * 128], KT.ap[:, jb * 128:(jb + 1) * 128],
                            QT.ap[:, n * 128:(n + 1) * 128], start=True, stop=True),
                            reads=[KT, QT], writes=[pS], inc=(s == slots[-1]))
                    kb.op("dve", lambda e, pS=pS, ein=ein, c0=c0, c1=c1, hd=hd: e.scalar_tensor_tensor(
                        out=ein.ap[:, c0:c1], in0=pS.ap[:, c0:c1], scalar=scaleA, in1=BIAS.ap[:, hd, c0:c1],
                        op0=ALU.mult, op1=ALU.add), reads=[pS, BIAS], writes=[ein])
                    kb.op("act", lambda e, ein=ein, ee=ee, c0=c0, c1=c1: e.activation(
                        out=ee.ap[:, c0:c1], in_=ein.ap[:, c0:c1], func=AF.Exp), reads=[ein], writes=[ee])
                    for s in slots:
                        jb = n - 1 + s
                        kb.mm(lambda e, s=s, jb=jb, pO=pO, ee=ee, slots=slots: e.matmul(
                            pO.ap[:, 0:128], Vt.ap[:, jb, :], ee.ap[:, s * 128:(s + 1) * 128],
                            start=(s == slots[0]), stop=(s == slots[-1])), reads=[Vt, ee], writes=[pO])
                    for s in slots:
                        kb.mm(lambda e, s=s, pO=pO, ee=ee, slots=slots: e.matmul(
                            pO.ap[:, 128:256], ONES_1.ap, ee.ap[:, s * 128:(s + 1) * 128],
                            start=(s == slots[0]), stop=(s == slots[-1])), reads=[ONES_1, ee], writes=[pO],
                            inc=(s == slots[-1]))
                    kb.op("dve", lambda e, pO=pO, den=den, hd=hd: e.tensor_scalar(
                        out=den.ap, in0=pO.ap[:, 128:256], scalar1=dr.ap[:, 128 + hd:129 + hd], scalar2=None,
                        op0=ALU.add), reads=[pO, dr], writes=[den])
                    kb.op("dve", lambda e, den=den: e.reciprocal(out=den.ap, in_=den.ap), reads=[den], writes=[den])
                    kb.op("dve", lambda e, pO=pO, den=den, n=n: e.tensor_tensor(
                        out=OA.ap[:, n * 128:(n + 1) * 128], in0=pO.ap[:, 0:128], in1=den.ap, op=ALU.mult),
                        reads=[pO, den], writes=[OA])
                kb.dma("sp", [(oT[hd * 128:(hd + 1) * 128, :], OA.ap)], reads=[OA])
            for hh in range(3):
                headA(g * 3 + hh)
            kb.barrier()
        for g in range(NKV):
            groupA(g)
        kb.scope(f"poolB{l}")
        UT = kb.alloc([8, T], BF16, dma=True)
        kb.dma("sp", [(UT.ap, projT[R_U:R_U + 1024, :].rearrange("(c p) t -> p c t", p=128))], writes=[UT])
        Ut = kb.alloc([16, 8, 128], BF16)
        PW = kb.alloc([4, 2, 256], BF16, dma=True)
        kb.dma("pool", [(PW.ap, pool_w[l].rearrange("g (cc p) d -> p g cc d", p=128))], writes=[PW])
        for sc in range(16):
            f = lambda i, sc=sc: UT.ap[:, i, sc * 128:(sc + 1) * 128]
            f.bufs = [UT]
            transpose_to(Ut, f, 8, PS[6 + sc % 2], dst_ap=Ut.ap[:, sc, :, :])
        MT = [kb.alloc([6, 2, 512], BF16, dma=True) for _ in range(2)]
        ZB = [kb.alloc([512], BF16) for _ in range(4)]
        OB = [kb.alloc([512], BF16, dma=True) for _ in range(4)]
        it = 0
        for g in range(4):
            for i in range(4):
                mt = MT[it % 2]
                kb.dma("act", [(mt.ap, mt_in[g, i].rearrange("p (k a t) -> p k a t", k=6, a=2))], writes=[mt])
                ks = [k for k in range(6) if 0 <= 4 * i - 1 + k < 16]
                for cc in range(2):
                    pz = PS[(it * 2 + cc) % 4]
                    zb = ZB[(it * 2 + cc) % 4]
                    seq = [(k, a) for k in ks for a in range(2)]
                    for (k, a) in seq:
                        sidx = 4 * i - 1 + k
                        kb.mm(lambda e, pz=pz, sidx=sidx, g=g, cc=cc, k=k, a=a, mt=mt, seq=seq: e.matmul(
                            pz.ap, Ut.ap[:, sidx, 2 * g + cc, :], mt.ap[:, k, a, :],
                            start=((k, a) == seq[0]), stop=((k, a) == seq[-1])),
                            reads=[Ut, mt], writes=[pz], inc=((k, a) == seq[-1]))
                    kb.op("act", lambda e, pz=pz, zb=zb: e.activation(out=zb.ap, in_=pz.ap, func=AF.Copy),
                          reads=[pz], writes=[zb])
                for dd in range(2):
                    po = PS[4 + (it * 2 + dd) % 4]
                    ob = OB[(it * 2 + dd) % 4]
                    for cc in range(2):
                        zb = ZB[(it * 2 + cc) % 4]
                        kb.mm(lambda e, po=po, g=g, cc=cc, dd=dd, zb=zb: e.matmul(
                            po.ap, PW.ap[:, g, cc, dd * 128:(dd + 1) * 128], zb.ap, start=(cc == 0), stop=(cc == 1)),
                            reads=[PW, zb], writes=[po], inc=(cc == 1))
                    kb.op("act", lambda e, po=po, ob=ob, g=g, dd=dd: e.activation(
                        out=ob.ap, in_=po.ap, func=AF.Identity, scale=v(l, V_PS + 2 * g + dd)),
                        reads=[po, VEC], writes=[ob])
                    row = 1536 + (2 * g + dd) * 128
                    kb.dma("sp", [(oT[row:row + 128, i * 512:(i + 1) * 512], ob.ap)], reads=[ob])
                it += 1
        kb.barrier()
        kb.scope(f"attnC{l}")
        def headC(hd, DIST, NBH):
            slope = 2.0 ** (-8.0 * (hd + 1) / NCH)
            kb.op("dve", lambda e, slope=slope: e.tensor_scalar(out=NBH.ap, in0=DIST.ap, scalar1=-slope, scalar2=-150.0,
                                                                op0=ALU.mult, op1=ALU.max), reads=[DIST], writes=[NBH])
            QT = kb.alloc([T], BF16, dma=True)
            KT = kb.alloc([T], BF16, dma=True)
            VT = kb.alloc([T], BF16, dma=True)
            Vt = kb.alloc([16, 128], BF16)
            OC = kb.alloc([T], BF16, dma=True)
            kb.dma("sp", [(QT.ap, projT[R_QC + hd * 128:R_QC + (hd + 1) * 128, :])], writes=[QT])
            kb.dma("sp", [(KT.ap, projT[R_KC + hd * 128:R_KC + (hd + 1) * 128, :])], writes=[KT])
            kb.dma("sp", [(VT.ap, projT[R_VC + hd * 128:R_VC + (hd + 1) * 128, :])], writes=[VT])
            f = lambda i: VT.ap[:, i * 128:(i + 1) * 128]
            f.bufs = [VT]
            transpose_to(Vt, f, 16, PS[7])
            EIN = [kb.alloc([512], F32) for _ in range(3)]
            E = [[kb.alloc([512], BF16) for _ in range(16)] for _ in range(2)]
            R12 = [kb.alloc([512], F32) for _ in range(2)]
            T12 = [kb.alloc([512], F32) for _ in range(2)]
            SQo = kb.alloc([512], BF16)
            RSo = kb.alloc([512], F32)
            cnt = 0
            for qt in range(4):
                q0 = qt * 512
                pO = [PS[3], PS[4]]
                pM = [PS[5], PS[6]]

                def pv(c, j):
                    kb.mm(lambda e, c=c, j=j: e.matmul(pO[c].ap, Vt.ap[:, j, :], E[c][j].ap,
                                                      start=(j == 0), stop=(j == 15)),
                          reads=[Vt, E[c][j]], writes=[pO[c]])
                    kb.mm(lambda e, c=c, j=j: e.matmul(pM[c].ap, ONES_1.ap, E[c][j].ap,
                                                      start=(j == 0), stop=(j == 15)),
                          reads=[ONES_1, E[c][j]], writes=[pM[c]], inc=(j == 15))
                seq = [(c, j) for c in range(2) for j in range(16)]
                LAG = 32
                for idx, (c, j) in enumerate(seq):
                    pS = PS[cnt % 3]
                    ein = EIN[cnt % 3]
                    cnt += 1
                    kb.mm(lambda e, pS=pS, c=c, j=j, q0=q0: e.matmul(
                        pS.ap, KT.ap[c * 64:(c + 1) * 64, j * 128:(j + 1) * 128],
                        QT.ap[c * 64:(c + 1) * 64, q0:q0 + 512], start=True, stop=True),
                        reads=[KT, QT], writes=[pS], inc=True)
                    d0 = q0 - j * 128 + DOFF
                    kb.op("dve", lambda e, pS=pS, ein=ein, d0=d0: e.scalar_tensor_tensor(
                        out=ein.ap, in0=pS.ap, scalar=scaleC, in1=NBH.ap[:, d0:d0 + 512],
                        op0=ALU.mult, op1=ALU.add), reads=[pS, NBH], writes=[ein])
                    kb.op("act", lambda e, ein=ein, ee=E[c][j]: e.activation(out=ee.ap, in_=ein.ap, func=AF.Exp),
                          reads=[ein], writes=[E[c][j]])
                    if idx >= LAG:
                        pv(*seq[idx - LAG])
                for k in range(LAG):
                    pv(*seq[len(seq) - LAG + k])
                for c in range(2):
                    kb.op("dve", lambda e, c=c: e.reciprocal(out=R12[c].ap, in_=pM[c].ap), reads=[pM[c]], writes=[R12[c]])
                    kb.op("dve", lambda e, c=c: e.tensor_tensor(out=T12[c].ap, in0=pO[c].ap, in1=R12[c].ap, op=ALU.mult),
                          reads=[pO[c], R12[c]], writes=[T12[c]])
                kb.op("dve", lambda e: e.scalar_tensor_tensor(out=T12[0].ap, in0=T12[1].ap, scalar=dr.ap[:, 148:149],
                                                              in1=T12[0].ap, op0=ALU.mult, op1=ALU.add),
                      reads=[T12[1], T12[0], dr], writes=[T12[0]])
                kb.op("act", lambda e: e.activation(out=SQo.ap, in_=T12[0].ap, func=AF.Square), reads=[T12[0]], writes=[SQo])
                kb.mm(lambda e: e.matmul(PS[7].ap, ONES_H.ap, SQo.ap, start=True, stop=True),
                      reads=[ONES_H, SQo], writes=[PS[7]], inc=True)
                kb.op("act", lambda e: e.activation(out=RSo.ap, in_=PS[7].ap, func=AF.Sqrt, bias=EPS.ap[:, 1:2], scale=1.0),
                      reads=[PS[7], EPS], writes=[RSo])
                kb.op("dve", lambda e: e.reciprocal(out=RSo.ap, in_=RSo.ap), reads=[RSo], writes=[RSo])
                kb.op("dve", lambda e: e.tensor_tensor(out=T12[1].ap, in0=T12[0].ap, in1=RSo.ap, op=ALU.mult),
                      reads=[T12[0], RSo], writes=[T12[1]])
                kb.op("act", lambda e, q0=q0: e.activation(out=OC.ap[:, q0:q0 + 512], in_=T12[1].ap, func=AF.Identity,
                                                          scale=dr.ap[:, 149:150]),
                      reads=[T12[1], dr], writes=[OC])
            kb.dma("sp", [(oT[2560 + hd * 128:2560 + (hd + 1) * 128, :], OC.ap)], reads=[OC])
        for hp in range(NCH // 2):
            DIST = kb.alloc([DW], F32, dma=True)
            kb.dma("sp", [(DIST.ap, dist_in)], writes=[DIST])
            NBH = kb.alloc([DW], F32)
            headC(2 * hp, DIST, NBH)
            headC(2 * hp + 1, DIST, NBH)
            kb.barrier()

    for l in range(depth):
        if l == 0:
            kb.scope("norm0")
            norm_stage(None, 0, None, xin, None, 0, 1)
        kb.scope(f"projgates{l}")
        for tt in range(2):
            sd = None
            if l == 0:
                sd = {"blocks": ada_blocks(0, 32 + 32 * tt, 64 + 32 * tt), "every": 2, "epi": ada_epi(0)}
            gemm_proj_gates(l, tt, side=sd)
        if l == 0:
            derive_rest(0)
            kb.barrier()
        mixer_stage(l)
        kb.scope(f"branch{l}")
        for tt in range(2):
            gemm_branch(l, tt)
        kb.scope(f"wout{l}")
        for tt in range(2):
            gemm_plain_f32(lambda m: W["w_out"][l][:, m * 256:(m + 1) * 256], 16,
                           mergedT[:, tt * 1024:(tt + 1) * 1024], KC, yT, tt * 1024)
        kb.scope(f"normA{l}")
        norm_stage(l, 1, yT, xin if l == 0 else xres, xres, l, 2)
        kb.scope(f"ffnup{l}")
        for tt in range(2):
            sd = None
            if l + 1 < depth:
                sd = {"blocks": ada_blocks(l + 1, 48 * tt, 48 * tt + 48), "every": 2, "epi": ada_epi(l + 1)}
            gemm_ffn_up(l, tt, side=sd)
        if l + 1 < depth:
            derive_pre1(l + 1)
            derive_rest(l + 1)
            kb.barrier()
        kb.scope(f"ffndown{l}")
        for tt in range(2):
            gemm_plain_f32(lambda m: W["ffn_w_down"][l][0:5504, m * 256:(m + 1) * 256], 16,
                           sT[0:5504, tt * 1024:(tt + 1) * 1024], 43, y1T, tt * 1024)
            gemm_plain_f32(lambda m: W["ffn_w_down"][l][5504:FF, m * 256:(m + 1) * 256], 16,
                           sT[5504:FF, tt * 1024:(tt + 1) * 1024], 43, yT, tt * 1024, add_src=y1T)
        last = (l == depth - 1)
        kb.scope(f"normB{l}")
        norm_stage(l, 2, yT, xres, out if last else xres, None if last else l + 1, 0 if last else 1)

    with nc.Block() as block:
        def run(name):
            def f(eng):
                cur = None
                for fn in kb.q[name]:
                    if isinstance(fn, tuple):
                        if PROFILE_SCOPES:
                            if cur is not None:
                                nc.leave_named_scope(cur[0], cur[1], False)
                            sid, _ = nc.enter_named_scope(fn[1], False)
                            cur = (fn[1], sid)
                        continue
                    fn(eng)
                if cur is not None:
                    nc.leave_named_scope(cur[0], cur[1], False)
            return f
        block.sync(run("sp"))
        block.scalar(run("act"))
        block.tensor(run("pe"))
        block.vector(run("dve"))
        block.gpsimd(run("pool"))
    return nc


def _consts():
    bf = ml_dtypes.bfloat16
    ident = np.eye(128, dtype=np.float32).astype(bf)
    i = np.arange(128)[:, None]
    m = np.arange(DW)[None, :]
    dist = np.abs(m - i - DOFF).astype(np.float32)
    slopes = 2.0 ** (-8.0 * np.arange(1, NA + 1) / NA)
    c = np.arange(384)[None, :]
    rel = (1 - c // 128) * 128 + (c % 128) - i
    ba = np.where((np.abs(rel) <= 128)[:, None, :], -slopes[None, :, None] * np.abs(rel)[:, None, :].astype(np.float64), -300.0)
    biasA = ba.astype(np.float32).reshape(128, NA * 384)
    mt = np.zeros((4, 4, 128, 6, 2, 512), dtype=bf)
    pos = np.arange(T)
    for g, w in enumerate(POOL_WINDOWS):
        r = w // 2
        lo = np.maximum(pos - r, 0)
        hi = np.minimum(pos + r + 1, T)
        cnt = (hi - lo).astype(np.float64)
        M = np.zeros((T, T), dtype=np.float64)
        for t in range(T):
            M[t, lo[t]:hi[t]] = 1.0 / cnt[t]
            M[t, t] -= 1.0
        M32 = M.astype(np.float32)
        Mh = M32.astype(bf)
        Ml = (M32 - Mh.astype(np.float32)).astype(bf)
        for it in range(4):
            for k in range(6):
                sidx = 4 * it - 1 + k
                if not (0 <= sidx < 16):
                    continue
                rows = slice(sidx * 128, (sidx + 1) * 128)
                colst = slice(it * 512, (it + 1) * 512)
                mt[g, it, :, k, 0, :] = Mh[colst, rows].T
                mt[g, it, :, k, 1, :] = Ml[colst, rows].T
    return ident, dist, biasA, mt.reshape(4, 4, 128, 6 * 2 * 512)


def _pc(vv):
    return np.ascontiguousarray(np.asarray(vv, dtype=np.float32).reshape(-1, 128).T)


def _make_in_maps(inputs, depth, cores):
    ident, dist, biasA, mt = _consts()
    shared = {"ident": ident, "dist": dist, "biasA": biasA, "mt": mt}
    for l in range(depth):
        for nm in ("ada_w", "w_in", "w_gate", "w_branch", "w_out", "ffn_w_gate", "ffn_w_up", "ffn_w_down"):
            shared[f"{nm}{l}"] = np.ascontiguousarray(inputs[nm][l])
        shared[f"pool_w{l}"] = np.ascontiguousarray(inputs["pool_w"][l])
    in_maps = []
    for b in cores:
        vec = np.zeros((128, NV), dtype=np.float32)
        for l in range(depth):
            o = l * VL
            vec[:, o + V_PRE1:o + V_PRE1 + 32] = _pc(inputs["mix_pre_g"][l])
            vec[:, o + V_POST1:o + V_POST1 + 32] = _pc(inputs["mix_post_g"][l])
            vec[:, o + V_PRE2:o + V_PRE2 + 32] = _pc(inputs["ffn_pre_g"][l])
            vec[:, o + V_POST2:o + V_POST2 + 32] = _pc(inputs["ffn_post_g"][l])
            vec[:, o + V_BG:o + V_BG + 96] = _pc(inputs["b_gate"][l])
            vec[:, o + V_PS:o + V_PS + 8] = _pc(inputs["pool_scale"][l])
            vec[:, o + V_AB:o + V_AB + 192] = _pc(inputs["ada_b"][l])
            vec[:, o + V_SUB:o + V_SUB + 1] = np.asarray(inputs["diff_subln_g"][l], dtype=np.float32).reshape(128, 1)
            vec[:, o + V_SINK:o + V_SINK + 12] = np.asarray(inputs["attn_sink"][l], dtype=np.float32)[None, :]
            vec[:, o + V_DL:o + V_DL + 256] = np.asarray(inputs["diff_lambda"][l], dtype=np.float32).reshape(1, 256)
        vec[:, V_C:V_C + 32] = _pc(inputs["c"][b])
        m = dict(shared)
        m["vecs"] = vec
        m["xT"] = np.ascontiguousarray(np.asarray(inputs["x"][b], dtype=np.float32).T)
        in_maps.append(m)
    return in_maps


def kernel(**inputs):
    nc = build_program(DEPTH)
    in_maps = _make_in_maps(inputs, DEPTH, list(range(NCORES)))
    res = run_bass_kernel_spmd(nc, in_maps, core_ids=list(range(NCORES)))
    outs = [np.ascontiguousarray(np.asarray(r["out"]).T) for r in res.results]
    return np.stack(outs, axis=0).astype(np.float32)
```
